# Optimizing a Trainium2 kernel written in Bass

```python
import jax, jax.numpy as jnp
from jax import lax
import numpy as np

D_MODEL = 1024
BATCH = 8
SEQ = 4096
DEPTH = 2

GRID_W = 64
CTX_LEN = 256
BRANCH = 256
MIX_WIDTH = 4 * BRANCH
CONV_K = 3
ATT_HEADS = 4
ATT_KV_HEADS = 2
HEAD_DIM = 64
ROPE_THETA = 10000.0
Q_BLOCK = 128
HG_HEADS = 4
HG_DK = BRANCH // HG_HEADS
HG_DV = BRANCH // HG_HEADS
HG_CHUNK = 64
FN_GROUPS = 4
FN_DIM = BRANCH // FN_GROUPS
EPS = 1e-6

IN_SPLITS = (256, 256, 256, 256,
             256, 128, 128, 256,
             256, 256, 256, 256, 256,
             256, 256)
IN_COLS = 3584

kernel_name = "hybrid_parallel_groups_flow_backbone"


def rms_norm(x, g):
    xf = x.astype(jnp.float32)
    y = xf * lax.rsqrt(jnp.mean(xf * xf, axis=-1, keepdims=True) + EPS)
    return (y * g.astype(jnp.float32)).astype(x.dtype)


def split_cols(t):
    points = [int(p) for p in np.cumsum(IN_SPLITS)[:-1]]
    return jnp.split(t, points, axis=-1)


def axial_rope_tables(rows, dtype):
    r, cidx = jnp.meshgrid(jnp.arange(rows), jnp.arange(GRID_W), indexing="ij")
    r = r.reshape(-1).astype(jnp.float32)
    cidx = cidx.reshape(-1).astype(jnp.float32)
    n_pairs = HEAD_DIM // 4
    freqs = ROPE_THETA ** (-jnp.arange(n_pairs, dtype=jnp.float32) / n_pairs)
    ang = jnp.concatenate([r[:, None] * freqs, cidx[:, None] * freqs], axis=-1)
    return jnp.cos(ang).astype(dtype), jnp.sin(ang).astype(dtype)


def apply_rope(x, cos, sin):
    x1, x2 = x[..., 0::2], x[..., 1::2]
    return jnp.stack([x1 * cos - x2 * sin, x1 * sin + x2 * cos], axis=-1).reshape(x.shape)


def to_heads(t, n_heads):
    b, n, _ = t.shape
    return t.reshape(b, n, n_heads, -1).transpose(0, 2, 1, 3)


def from_heads(t):
    b, h, n, d = t.shape
    return t.transpose(0, 2, 1, 3).reshape(b, n, h * d)


def short_conv(u, w):
    up = jnp.pad(u, ((0, 0), (1, 1), (0, 0)))
    return up[:, :-2] * w[0] + up[:, 1:-1] * w[1] + up[:, 2:] * w[2]


def attend(qb, k, v):
    s = jnp.einsum("bhgqd,bhkd->bhgqk", qb, k).astype(jnp.float32) * (HEAD_DIM ** -0.5)
    p = jax.nn.softmax(s, axis=-1).astype(v.dtype)
    return jnp.einsum("bhgqk,bhkd->bhgqd", p, v)


def gqa_branch(q_l, k_l, v_l, q_c, k_c, v_c, q_g, k_g, cos, sin, need_ctx):
    b, n, _ = q_l.shape
    g = ATT_HEADS // ATT_KV_HEADS
    ql = apply_rope(rms_norm(to_heads(q_l, ATT_HEADS), q_g), cos, sin)
    kl = apply_rope(rms_norm(to_heads(k_l, ATT_KV_HEADS), k_g), cos, sin)
    vl = to_heads(v_l, ATT_KV_HEADS)
    kc = rms_norm(to_heads(k_c, ATT_KV_HEADS), k_g)
    vc = to_heads(v_c, ATT_KV_HEADS)
    k_all = jnp.concatenate([kl, kc], axis=2)
    v_all = jnp.concatenate([vl, vc], axis=2)
    nb = n // Q_BLOCK
    qb = ql.reshape(b, ATT_KV_HEADS, g, nb, Q_BLOCK, HEAD_DIM).transpose(3, 0, 1, 2, 4, 5)
    ob = lax.map(lambda blk: attend(blk, k_all, v_all), qb)
    o_lat = ob.transpose(1, 0, 4, 2, 3, 5).reshape(b, n, ATT_HEADS * HEAD_DIM)
    o_ctx = None
    if need_ctx:
        nc = q_c.shape[1]
        qc = rms_norm(to_heads(q_c, ATT_HEADS), q_g).reshape(b, ATT_KV_HEADS, g, nc, HEAD_DIM)
        oc = attend(qc, kc, vc)
        o_ctx = oc.transpose(0, 3, 1, 2, 4).reshape(b, nc, ATT_HEADS * HEAD_DIM)
    return o_lat, o_ctx


def hgrn_lower_bounds(lb_param):
    p = jax.nn.softmax(lb_param.astype(jnp.float32), axis=1)
    cs = jnp.cumsum(p, axis=1)
    return cs - cs[:, :1]


def hgrn_gates(fx, lb):
    fx = fx.astype(jnp.float32)
    log_f = jnp.logaddexp(jnp.log(lb), jnp.log1p(-lb) + jax.nn.log_sigmoid(fx))
    k = (1.0 - lb) * jax.nn.sigmoid(-fx)
    return log_f, k


def hgrn_scan(q, k, v, log_f, s0, with_output):
    b, h, t, _ = q.shape
    n = t // HG_CHUNK

    def chunks(a):
        return a.astype(jnp.float32).reshape(b, h, n, HG_CHUNK, a.shape[-1]).transpose(2, 0, 1, 3, 4)

    causal = jnp.tril(jnp.ones((HG_CHUNK, HG_CHUNK), dtype=bool))[:, :, None]

    def step(state, inp):
        qc, kc, vc, lf = inp
        a = jnp.cumsum(lf, axis=2)
        a_last = a[:, :, -1:, :]
        s_new = jnp.exp(a_last)[:, :, 0, :, None] * state + jnp.einsum(
            "bhsk,bhsv->bhkv", kc * jnp.exp(a_last - a), vc)
        if not with_output:
            return s_new, None
        inter = jnp.einsum("bhtk,bhkv->bhtv", qc * jnp.exp(a), state)
        diff = a[:, :, :, None, :] - a[:, :, None, :, :]
        dec = jnp.exp(jnp.where(causal, diff, -jnp.inf))
        scores = jnp.einsum("bhtk,bhsk,bhtsk->bhts", qc, kc, dec)
        intra = jnp.einsum("bhts,bhsv->bhtv", scores, vc)
        return s_new, inter + intra

    s_fin, out = lax.scan(step, s0, (chunks(q), chunks(k), chunks(v), chunks(log_f)))
    if with_output:
        out = out.transpose(1, 2, 0, 3, 4).reshape(b, h, t, v.shape[-1])
    return s_fin, out


def hgrn_branch(q_l, ff_l, fb_l, i_l, q_c, ff_c, fb_c, i_c, lb, hg_g, need_ctx):
    b = q_l.shape[0]
    ql = to_heads(jax.nn.silu(q_l), HG_HEADS)
    il = to_heads(i_l, HG_HEADS)
    qc = to_heads(jax.nn.silu(q_c), HG_HEADS)
    ic = to_heads(i_c, HG_HEADS)
    o_lat, o_ctx = 0.0, 0.0
    for d, (f_l, f_c) in enumerate(((ff_l, ff_c), (fb_l, fb_c))):
        flip = (lambda a: a[:, :, ::-1]) if d == 1 else (lambda a: a)
        logf_l, k_l = hgrn_gates(f_l, lb[d])
        logf_c, k_c = hgrn_gates(f_c, lb[d])
        s0 = jnp.zeros((b, HG_HEADS, HG_DK, HG_DV), jnp.float32)
        s_ctx, oc = hgrn_scan(flip(qc), flip(to_heads(k_c, HG_HEADS)), flip(ic),
                              flip(to_heads(logf_c, HG_HEADS)), s0, need_ctx)
        _, ol = hgrn_scan(flip(ql), flip(to_heads(k_l, HG_HEADS)), flip(il),
                          flip(to_heads(logf_l, HG_HEADS)), s_ctx, True)
        o_lat = o_lat + flip(ol)
        if need_ctx:
            o_ctx = o_ctx + flip(oc)
    gain = hg_g.reshape(HG_HEADS, 1, HG_DV)
    out_l = from_heads(rms_norm(o_lat, gain)).astype(q_l.dtype)
    out_c = from_heads(rms_norm(o_ctx, gain)).astype(q_l.dtype) if need_ctx else None
    return out_l, out_c


def fourier_mix(u):
    b, t, _ = u.shape
    uf = u.astype(jnp.float32).reshape(b, t, FN_GROUPS, FN_DIM)
    y = jnp.fft.fft2(uf, axes=(1, 3), norm="ortho").real
    return y.reshape(b, t, BRANCH).astype(u.dtype)


def mix_stream(parts, conv_w):
    cb, cc, cv, cz, _, _, _, az, _, _, _, _, hz, fu, fz = parts
    y_conv = cb * short_conv(cc * cv, conv_w) * jax.nn.silu(cz)
    y_four = fourier_mix(fu) * jax.nn.silu(fz)
    return y_conv, y_four, jax.nn.silu(az), jax.nn.silu(hz)


def hybrid_layer(h, hc, c_act, cc_act, norm_g, w_mod, b_mod, w_in, conv_w, q_g, k_g, lb, hg_g,
                 w_out, cos, sin, need_ctx):
    shift, scale, gate = jnp.split((c_act @ w_mod + b_mod)[:, None, :], 3, axis=-1)
    shift_c, scale_c, gate_c = jnp.split(cc_act @ w_mod + b_mod, 3, axis=-1)
    lat = split_cols((rms_norm(h, norm_g) * (1.0 + scale) + shift) @ w_in)
    cx = split_cols((rms_norm(hc, norm_g) * (1.0 + scale_c) + shift_c) @ w_in)

    att_l, att_c = gqa_branch(lat[4], lat[5], lat[6], cx[4], cx[5], cx[6], q_g, k_g, cos, sin, need_ctx)
    hg_l, hg_c = hgrn_branch(lat[8], lat[9], lat[10], lat[11], cx[8], cx[9], cx[10], cx[11],
                             lb, hg_g, need_ctx)

    y_conv, y_four, g_att, g_hg = mix_stream(lat, conv_w)
    y = jnp.concatenate([y_conv, att_l * g_att, hg_l * g_hg, y_four], axis=-1) @ w_out
    h_new = h + gate * y
    hc_new = None
    if need_ctx:
        yc_conv, yc_four, gc_att, gc_hg = mix_stream(cx, conv_w)
        yc = jnp.concatenate([yc_conv, att_c * gc_att, hg_c * gc_hg, yc_four], axis=-1) @ w_out
        hc_new = hc + gate_c * yc
    return h_new, hc_new


def setup_inputs(seed: int = 0) -> dict:
    key = jax.random.key(seed)
    ks = jax.random.split(key, 16)
    f32 = jnp.float32
    nrm = lambda k, shape: jax.random.normal(k, shape, f32)
    return {
        "x": nrm(ks[0], (BATCH, SEQ, D_MODEL)),
        "c": nrm(ks[1], (BATCH, D_MODEL)),
        "ctx": nrm(ks[2], (BATCH, CTX_LEN, D_MODEL)),
        "c_ctx": nrm(ks[3], (D_MODEL,)),
        "norm_g": 1.0 + 0.1 * nrm(ks[4], (DEPTH, D_MODEL)),
        "w_mod": nrm(ks[5], (DEPTH, D_MODEL, 3 * D_MODEL)) * (0.5 * D_MODEL ** -0.5),
        "b_mod": 0.02 * nrm(ks[6], (DEPTH, 3 * D_MODEL)),
        "w_in": nrm(ks[7], (DEPTH, D_MODEL, IN_COLS)) * (D_MODEL ** -0.5),
        "conv_w": nrm(ks[8], (DEPTH, CONV_K, BRANCH)) * (CONV_K ** -0.5),
        "q_norm_g": 1.0 + 0.1 * nrm(ks[9], (DEPTH, HEAD_DIM)),
        "k_norm_g": 1.0 + 0.1 * nrm(ks[10], (DEPTH, HEAD_DIM)),
        "hgrn_lb": nrm(ks[11], (2, DEPTH, BRANCH)),
        "hgrn_norm_g": 1.0 + 0.1 * nrm(ks[12], (DEPTH, BRANCH)),
        "w_out": nrm(ks[13], (DEPTH, MIX_WIDTH, D_MODEL)) * (MIX_WIDTH ** -0.5),
        "final_g": 1.0 + 0.1 * nrm(ks[14], (D_MODEL,)),
    }


def reference(x, c, ctx, c_ctx, norm_g, w_mod, b_mod, w_in, conv_w, q_norm_g, k_norm_g,
              hgrn_lb, hgrn_norm_g, w_out, final_g):
    rows = x.shape[1] // GRID_W
    cos, sin = axial_rope_tables(rows, x.dtype)
    lb_all = hgrn_lower_bounds(hgrn_lb)
    c_act = jax.nn.silu(c)
    cc_act = jax.nn.silu(c_ctx)
    h, hc = x, ctx
    for layer in range(DEPTH):
        h, hc = hybrid_layer(h, hc, c_act, cc_act, norm_g[layer], w_mod[layer], b_mod[layer],
                             w_in[layer], conv_w[layer], q_norm_g[layer], k_norm_g[layer],
                             lb_all[:, layer], hgrn_norm_g[layer], w_out[layer], cos, sin,
                             need_ctx=layer < DEPTH - 1)
    return rms_norm(h, final_g)
```

```python
import numpy as np
import ml_dtypes
import concourse.bass as bass
import concourse.mybir as mybir
from concourse.bass_utils import run_bass_kernel_spmd

F32 = mybir.dt.float32
BF16 = mybir.dt.bfloat16
I32 = mybir.dt.int32
AF = mybir.ActivationFunctionType
ALU = mybir.AluOpType
AX = mybir.AxisListType

D = 1024
S = 4096
L = 256
TT = S + L
NT = TT // 128
DEPTH = 2
NF = 2432
EPS = 1e-6
F_CB, F_CC, F_CV, F_CZ, F_Q, F_K, F_AZ, F_HQ, F_HZ, F_FZ = 0, 256, 512, 768, 1024, 1280, 1408, 1664, 1920, 2176
O_CB, O_CC, O_CV, O_CZ, O_Q, O_K, O_V, O_AZ, O_HQ, O_FF, O_FB, O_HI, O_HZ, O_FU, O_FZ = (
    0, 256, 512, 768, 1024, 1280, 1408, 1536, 1792, 2048, 2304, 2560, 2816, 3072, 3328)


class Res:
    __slots__ = ("w", "u", "r", "name")

    def __init__(self, name=""):
        self.w = {}
        self.u = {}
        self.r = {}
        self.name = name


class _Q:
    def __init__(self, name, sem, key):
        self.name = name
        self.sem = sem
        self.key = key
        self.count = 0
        self.ops = []
        self.known = {}


class Sched:
    def __init__(self, nc, ndma=8):
        self.nc = nc
        self.sems = {}
        self.q = {}
        for i, n in enumerate(["pe", "act", "dve", "pool", "sp"]):
            sem = nc.alloc_semaphore("q_" + n)
            self.sems[n] = sem
            self.q[n] = _Q(n, sem, n)
        self.dpool = {}
        self.di = {}
        for qn in ["sp", "pool", "act"]:
            lst = []
            for i in range(ndma):
                key = "d_%s_%d" % (qn, i)
                self.sems[key] = nc.alloc_semaphore(key)
                lst.append([key, 0])
            self.dpool[qn] = lst
            self.di[qn] = 0

    def _wait(self, q, need):
        for k, v in need.items():
            if q.known.get(k, 0) >= v:
                continue
            q.known[k] = v
            q.ops.append(("w", k, v))

    @staticmethod
    def _add(need, d):
        for k, v in d.items():
            if need.get(k, 0) < v:
                need[k] = v

    def _hazards(self, reads, writes, uw):
        need = {}
        for r in reads:
            self._add(need, r.w)
            self._add(need, r.u)
        for r in writes:
            self._add(need, r.w)
            self._add(need, r.u)
            self._add(need, r.r)
        for r in uw:
            self._add(need, r.w)
            self._add(need, r.r)
        return need

    def _commit(self, k, v, reads, writes, uw):
        for r in reads:
            if r.r.get(k, 0) < v:
                r.r[k] = v
        for r in writes:
            r.w = {k: v}
            r.u = {}
            r.r = {}
        for r in uw:
            if r.u.get(k, 0) < v:
                r.u[k] = v
            r.r = {}

    def op(self, qn, fn, reads=(), writes=(), uw=()):
        q = self.q[qn]
        need = self._hazards(reads, writes, uw)
        if qn == "pe":
            need.pop("pe", None)
        self._wait(q, need)
        q.count += 1
        q.ops.append(("i", fn))
        self._commit(q.key, q.count, reads, writes, uw)

    def dma(self, qn, out, in_, reads=(), writes=(), uw=(), **kw):
        q = self.q[qn]
        pool = self.dpool[qn]
        slot = pool[self.di[qn] % len(pool)]
        self.di[qn] += 1
        need = self._hazards(reads, writes, uw)
        if slot[1] > 0:
            self._add(need, {slot[0]: slot[1]})
        self._wait(q, need)
        slot[1] += 16
        q.ops.append(("d", out, in_, kw, slot[0]))
        self._commit(slot[0], slot[1], reads, writes, uw)

    def barrier(self):
        allev = {}
        for n, q in self.q.items():
            if q.count > 0:
                allev[n] = q.count
        for qn, pool in self.dpool.items():
            for key, val in pool:
                if val > 0:
                    allev[key] = val
        for n, q in self.q.items():
            need = dict(allev)
            need.pop(n, None)
            self._wait(q, need)

    def emit(self):
        nc = self.nc
        sems = self.sems

        def run(q, e):
            for o in q.ops:
                if o[0] == "w":
                    e.wait_ge(sems[o[1]], o[2])
                elif o[0] == "i":
                    o[1](e).then_inc(q.sem, 1)
                else:
                    e.dma_start(out=o[1], in_=o[2], **o[3]).then_inc(sems[o[4]], 16)

        with nc.Block() as block:
            @block.tensor
            def _(e):
                run(self.q["pe"], e)

            @block.scalar
            def _(e):
                run(self.q["act"], e)

            @block.vector
            def _(e):
                run(self.q["dve"], e)

            @block.gpsimd
            def _(e):
                run(self.q["pool"], e)

            @block.sync
            def _(e):
                run(self.q["sp"], e)


class Arena:
    def __init__(self, nc, lo, hi):
        self.nc = nc
        self.lo = lo
        self.hi = hi
        self.cur = lo
        self.n = 0

    def reset(self):
        self.cur = self.lo

    def alloc(self, shape, dtype, name="t"):
        free = 1
        for s in shape[1:]:
            free *= s
        esz = 4 if dtype in (F32, I32) else 2
        nbytes = free * esz
        off = (self.cur + 63) // 64 * 64
        assert off + nbytes <= self.hi, "SBUF arena overflow %s %d+%d>%d" % (name, off, nbytes, self.hi)
        self.cur = off + nbytes
        self.n += 1
        h = self.nc.alloc_sbuf_tensor_at("%s_%d" % (name, self.n), [shape[0], free], dtype, offset=off)
        ap = h.ap()
        if len(shape) == 3:
            ap = ap.rearrange("p (a b) -> p a b", a=shape[1])
        elif len(shape) == 4:
            ap = ap.rearrange("p (a b c) -> p a b c", a=shape[1], b=shape[2])
        return ap


def _bf(a):
    return np.ascontiguousarray(a.astype(ml_dtypes.bfloat16))


_CONST_CACHE = {}


def host_constants():
    if _CONST_CACHE:
        return _CONST_CACHE
    c = {}
    c["ident_f"] = np.eye(128, dtype=np.float32)
    c["ident_b"] = _bf(np.eye(128, dtype=np.float32))
    bo = np.zeros((128, 128), np.float32)
    bo[:64, :64] = 1.0 / 64
    bo[64:, 64:] = 1.0 / 64
    c["blkavg"] = bo
    pt = np.zeros((128, 128), np.float32)
    for i in range(64):
        pt[2 * i + 1, 2 * i] = -1.0
        pt[2 * i, 2 * i + 1] = 1.0
    c["ropeP"] = pt
    t = np.arange(S)
    r = (t // 64).astype(np.float32)
    cc = (t % 64).astype(np.float32)
    freqs = (10000.0 ** (-np.arange(16, dtype=np.float32) / 16)).astype(np.float32)
    ang = np.concatenate([r[:, None] * freqs, cc[:, None] * freqs], axis=-1).astype(np.float32)
    cos = np.cos(ang).astype(np.float32)
    sin = np.sin(ang).astype(np.float32)
    cosT = np.ones((128, TT), np.float32)
    sinT = np.zeros((128, TT), np.float32)
    for d in range(64):
        cosT[d, :S] = cos[:, d // 2]
        sinT[d, :S] = sin[:, d // 2]
    cosT[64:] = cosT[:64]
    sinT[64:] = sinT[:64]
    c["cosT"] = cosT
    c["sinT"] = sinT
    s_ = np.arange(128)[:, None]
    t_ = np.arange(128)[None, :]
    same = (s_ // 64) == (t_ // 64)
    MID = 31
    midt = (t_ // 64) * 64 + MID
    dm_f = (same & (s_ > t_)).astype(np.float32)
    dm_b = (same & (s_ < t_)).astype(np.float32)
    am_f = (same & (s_ <= t_)).astype(np.float32) - (same & (s_ <= midt)).astype(np.float32)
    am_b = (same & (s_ >= t_)).astype(np.float32) - (same & (s_ >= midt)).astype(np.float32)
    c["hg_dm"] = np.stack([dm_f, dm_b])
    c["hg_am"] = np.stack([am_f, am_b])
    ind = np.zeros((2, 128, 32), np.float32)
    sl = np.arange(128)
    for cl in range(2):
        inch = (sl // 64) == cl
        ind[0, :, cl] = inch
        ind[1, :, cl] = inch
        ind[0, :, 2 + cl] = inch & (sl % 64 <= MID)
        ind[1, :, 2 + cl] = inch & (sl % 64 >= MID)
    c["hg_ind"] = ind
    s6 = np.arange(64)[:, None]
    t6 = np.arange(64)[None, :]
    mk_f = (s6 <= t6).astype(np.int32)
    mk_b = (s6 >= t6).astype(np.int32)
    mk = np.zeros((128, 2, 2, 128), np.float32)
    for cl in range(2):
        for pr in range(2):
            mk[cl * 64:(cl + 1) * 64, 0, pr, cl * 64:(cl + 1) * 64] = mk_f
            mk[cl * 64:(cl + 1) * 64, 1, pr, cl * 64:(cl + 1) * 64] = mk_b
    c["hg_mask"] = np.ascontiguousarray(mk.reshape(128, 512))
    rm = np.zeros((128, 2), np.float32)
    rm[:64, 0] = 1.0
    rm[64:, 1] = 1.0
    c["hg_rowm"] = rm
    tl = np.arange(128) % 64
    cm = np.zeros((128, 4, 128), np.float32)
    cm[:, 0:2, :] = (tl < 32)[None, None, :]
    cm[:, 2:4, :] = (tl >= 32)[None, None, :]
    c["hg_cm"] = _bf(cm.reshape(128, 512))
    c["ones64"] = np.full((64, 64), 1.0 / 64, np.float32)
    tt = np.arange(S, dtype=np.int64)
    m = (tt[:, None] * tt[None, :]) % S
    ang = (2.0 * np.pi / S) * m.astype(np.float64)
    c["dft_c"] = _bf(np.cos(ang))
    c["dft_s"] = _bf(np.sin(ang))
    tl = np.arange(L, dtype=np.int64)
    m = (tl[:, None] * tl[None, :]) % L
    ang = (2.0 * np.pi / L) * m.astype(np.float64)
    c["dftc_c"] = _bf(np.cos(ang))
    c["dftc_s"] = _bf(np.sin(ang))
    d_ = np.arange(64)
    a64 = (2.0 * np.pi / 64) * ((d_[:, None] * d_[None, :]) % 64).astype(np.float64)
    cb = np.zeros((128, 128))
    sb = np.zeros((128, 128))
    for g in range(2):
        cb[g * 64:(g + 1) * 64, g * 64:(g + 1) * 64] = np.cos(a64)
        sb[g * 64:(g + 1) * 64, g * 64:(g + 1) * 64] = -np.sin(a64)
    c["ch_c"] = _bf(cb)
    c["ch_s"] = _bf(sb)
    _CONST_CACHE.update(c)
    return c


CONST_SPECS = [
    ("ident_f", [128, 128], F32), ("ident_b", [128, 128], BF16), ("blkavg", [128, 128], F32),
    ("ropeP", [128, 128], F32), ("cosT", [128, TT], F32), ("sinT", [128, TT], F32),
    ("hg_dm", [2, 128, 128], F32), ("hg_am", [2, 128, 128], F32), ("hg_ind", [2, 128, 32], F32),
    ("hg_mask", [128, 512], F32), ("hg_rowm", [128, 2], F32), ("hg_cm", [128, 512], BF16), ("ones64", [64, 64], F32),
    ("dft_c", [S, S], BF16), ("dft_s", [S, S], BF16), ("dftc_c", [L, L], BF16), ("dftc_s", [L, L], BF16),
    ("ch_c", [128, 128], BF16), ("ch_s", [128, 128], BF16),
]

IN_SPECS = [
    ("xc", [TT, D], F32),
    ("cvec", [128, 2, 8], F32),
    ("w_mod", [DEPTH, D, 3 * D], F32),
    ("b_mod_b", [DEPTH, 128, 3 * D], F32),
    ("norm_g_b", [DEPTH, 128, D], F32),
    ("w_in_p", [DEPTH, D, 3584], F32),
    ("conv_w_c", [DEPTH, 128, 2, 3], F32),
    ("qk_g_c", [DEPTH, 128, 2], F32),
    ("lbp_b", [128, 2, 2, 256], F32),
    ("hg_g_c", [DEPTH, 64, 4], F32),
    ("w_out", [DEPTH, D, D], F32),
    ("final_g_b", [128, D], F32),
]


def prepare_inputs(inp):
    perm = np.concatenate([
        np.arange(O_CB, O_CB + 1024),
        np.arange(O_Q, O_Q + 256), np.arange(O_K, O_K + 128), np.arange(O_AZ, O_AZ + 256),
        np.arange(O_HQ, O_HQ + 256), np.arange(O_HZ, O_HZ + 256), np.arange(O_FZ, O_FZ + 256),
        np.arange(O_FF, O_FF + 512),
        np.arange(O_V, O_V + 128), np.arange(O_HI, O_HI + 256), np.arange(O_FU, O_FU + 256)])
    assert perm.shape[0] == 3584
    f = np.float32
    shared = {}
    shared["w_mod"] = np.ascontiguousarray(inp["w_mod"], dtype=f)
    shared["b_mod_b"] = np.ascontiguousarray(np.broadcast_to(inp["b_mod"][:, None, :], (DEPTH, 128, 3 * D)), dtype=f)
    shared["norm_g_b"] = np.ascontiguousarray(np.broadcast_to(inp["norm_g"][:, None, :], (DEPTH, 128, D)), dtype=f)
    shared["w_in_p"] = np.ascontiguousarray(inp["w_in"][:, :, perm], dtype=f)
    cw = inp["conv_w"]
    shared["conv_w_c"] = np.ascontiguousarray(cw.reshape(DEPTH, 3, 2, 128).transpose(0, 3, 2, 1), dtype=f)
    qg = np.concatenate([inp["q_norm_g"], inp["q_norm_g"]], axis=1)
    kg = np.concatenate([inp["k_norm_g"], inp["k_norm_g"]], axis=1)
    shared["qk_g_c"] = np.ascontiguousarray(np.stack([qg, kg], axis=-1), dtype=f)
    shared["lbp_b"] = np.ascontiguousarray(np.broadcast_to(inp["hgrn_lb"][None], (128, 2, DEPTH, 256)), dtype=f)
    hg = inp["hgrn_norm_g"].reshape(DEPTH, 4, 64).transpose(0, 2, 1)
    shared["hg_g_c"] = np.ascontiguousarray(hg, dtype=f)
    shared["w_out"] = np.ascontiguousarray(inp["w_out"], dtype=f)
    shared["final_g_b"] = np.ascontiguousarray(np.broadcast_to(inp["final_g"][None, :], (128, D)), dtype=f)
    shared.update(host_constants())
    maps = []
    for b in range(8):
        m = dict(shared)
        m["xc"] = np.ascontiguousarray(np.concatenate([inp["x"][b], inp["ctx"][b]], axis=0), dtype=f)
        cv = np.stack([inp["c"][b], inp["c_ctx"]], axis=0)
        m["cvec"] = np.ascontiguousarray(cv.reshape(2, 8, 128).transpose(2, 0, 1), dtype=f)
        maps.append(m)
    return maps


P_START = 17 * 1024
P_END = 61 * 1024
A_END = 222 * 1024


class Prog:
    def __init__(self, layers=(0, 1), phases=("mod", "proj", "conv", "att", "hg", "four", "out"), dbg=()):
        self.nc = nc = bass.Bass("TRN2", target_bir_lowering=False)
        self.layers = layers
        self.phases = phases
        self.T = {}
        for name, shape, dt in IN_SPECS + CONST_SPECS:
            self.T[name] = nc.dram_tensor(name, shape, dt, kind="ExternalInput").ap()
        self.out = nc.dram_tensor("out", [S, D], F32, kind="ExternalOutput").ap()
        kind = "ExternalOutput" if dbg else "Internal"
        self.PJF = nc.dram_tensor("PJF", [NF, TT], F32, kind=kind).ap()
        self.PJTf = nc.dram_tensor("PJTf", [TT, 512], F32, kind=kind).ap()
        self.PJTb = nc.dram_tensor("PJTb", [TT, 640], BF16, kind=kind).ap()
        self.MIX = nc.dram_tensor("MIX", [D, TT], BF16, kind=kind).ap()
        self.H1 = nc.dram_tensor("H1", [TT, D], F32, kind=kind).ap()
        self.dbg = dbg
        self.rPJF, self.rPJTf, self.rPJTb, self.rMIX, self.rH1, self.rOUT = (Res(n) for n in
                                                                              ["PJF", "PJTf", "PJTb", "MIX", "H1", "OUT"])
        self.rIN = Res("in")
        self.S = Sched(nc)
        self.PS = [nc.alloc_psum_tensor("ps%d" % i, [128, 512], F32).ap() for i in range(8)]
        self.rPS = [Res("ps%d" % i) for i in range(8)]
        self.pers = Arena(nc, P_START, P_END)
        self.ar = Arena(nc, P_END, A_END)
        self._consts()
        for l in layers:
            for ph in phases:
                getattr(self, "ph_" + ph)(l)
                self.S.barrier()
                self.ar.reset()
        self.S.barrier()
        self.S.emit()

    def tile(self, shape, dt, name="t", pers=False):
        ap = (self.pers if pers else self.ar).alloc(shape, dt, name)
        return ap, Res(name)

    def rot(self, n, shape, dt, name="r"):
        tiles = [self.tile(shape, dt, name) for _ in range(n)]
        st = {"i": 0}

        def nxt():
            t = tiles[st["i"] % n]
            st["i"] += 1
            return t
        return nxt

    def load(self, dst, rdst, src, rsrc=None, q="sp", **kw):
        self.S.dma(q, dst, src, reads=[rsrc or self.rIN], writes=[rdst], **kw)

    def store(self, dst, rdst, src, rsrc, q="pool", **kw):
        self.S.dma(q, dst, src, reads=[rsrc], uw=[rdst], **kw)

    def _consts(self):
        T = self.T
        self.C = {}
        for name, shape, dt in [("ident_f", [128, 128], F32), ("ident_b", [128, 128], BF16),
                                ("blkavg", [128, 128], F32), ("ropeP", [128, 128], F32),
                                ("hg_mask", [128, 512], F32), ("hg_rowm", [128, 2], F32), ("hg_cm", [128, 512], BF16), ("ch_c", [128, 128], BF16), ("ch_s", [128, 128], BF16)]:
            ap, r = self.tile(shape, dt, name, pers=True)
            self.load(ap, r, T[name])
            self.C[name] = (ap, r)
        for name in ["hg_dm", "hg_am"]:
            ap, r = self.tile([128, 2, 128], F32, name, pers=True)
            self.load(ap, r, T[name].rearrange("a p c -> p a c"))
            self.C[name] = (ap, r)
        ap, r = self.tile([128, 2, 32], F32, "hg_ind", pers=True)
        self.load(ap, r, T["hg_ind"].rearrange("a p c -> p a c"))
        self.C["hg_ind"] = (ap, r)
        ap, r = self.tile([64, 64], F32, "ones64", pers=True)
        self.load(ap, r, T["ones64"])
        self.C["ones64"] = (ap, r)
        ap, r = self.tile([128, 64], F32, "onesf", pers=True)
        self.S.op("pool", lambda e: e.memset(ap, 1.0), writes=[r])
        self.C["onesf"] = (ap, r)
        self.gs = [self.tile([128, D], F32, "gs", pers=True) for _ in range(2)]
        self.sh = [self.tile([128, D], F32, "sh", pers=True) for _ in range(2)]
        self.gt = [self.tile([128, D], F32, "gt", pers=True) for _ in range(2)]

    def ph_mod(self, l):
        S_, T, PS, rPS = self.S, self.T, self.PS, self.rPS
        cv, rcv = self.tile([128, 2, 8], F32, "cv")
        self.load(cv, rcv, T["cvec"])
        ca, rca = self.tile([128, 2, 8], F32, "ca")
        S_.op("act", lambda e: e.activation(out=ca, in_=cv, func=AF.Silu), reads=[rcv], writes=[rca])
        cab, rcab = self.tile([128, 2, 8, 128], F32, "cab")
        for s in range(2):
            S_.op("dve", lambda e, s=s: e.tensor_copy(out=cab[:, s], in_=ca[:, s].unsqueeze(2).broadcast_to([128, 8, 128])),
                  reads=[rca], writes=[rcab])
        bmb, rbmb = self.tile([128, 3 * D], F32, "bmb")
        self.load(bmb, rbmb, T["b_mod_b"][l])
        ngb, rngb = self.tile([128, D], F32, "ngb")
        self.load(ngb, rngb, T["norm_g_b"][l])
        modv = [self.tile([128, 3 * D], F32, "modv") for _ in range(2)]
        nxt = self.rot(2, [128, 8, 512], F32, "wm")
        wsrc = T["w_mod"][l].rearrange("(kt p) c -> p kt c", p=128)
        for blk in range(6):
            wm, rwm = nxt()
            self.load(wm, rwm, wsrc[:, :, blk * 512:(blk + 1) * 512])
            for s in range(2):
                pi = (blk * 2 + s) % 4
                for kt in range(8):
                    S_.op("pe", lambda e, s=s, kt=kt, pi=pi, wm=wm: e.matmul(PS[pi], lhsT=cab[:, s, kt, :], rhs=wm[:, kt, :],
                                                                          start=(kt == 0), stop=(kt == 7)),
                          reads=[rcab, rwm], writes=[rPS[pi]])
                mv, rmv = modv[s]
                S_.op("dve", lambda e, pi=pi, mv=mv, blk=blk: e.tensor_tensor(
                    out=mv[:, blk * 512:(blk + 1) * 512], in0=PS[pi], in1=bmb[:, blk * 512:(blk + 1) * 512], op=ALU.add),
                    reads=[rPS[pi], rbmb], writes=[rmv])
        for s in range(2):
            mv, rmv = modv[s]
            S_.op("act", lambda e, s=s, mv=mv: e.copy(out=self.sh[s][0], in_=mv[:, 0:D]), reads=[rmv], writes=[self.sh[s][1]])
            S_.op("dve", lambda e, s=s, mv=mv: e.scalar_tensor_tensor(out=self.gs[s][0], in0=mv[:, D:2 * D], scalar=1.0, in1=ngb,
                                                                      op0=ALU.add, op1=ALU.mult),
                  reads=[rmv, rngb], writes=[self.gs[s][1]])
            S_.op("act", lambda e, s=s, mv=mv: e.copy(out=self.gt[s][0], in_=mv[:, 2 * D:3 * D]), reads=[rmv], writes=[self.gt[s][1]])

    def rstd_of(self, ht, rht, junk, rjunk, ss, rss, rs, rrs):
        S_ = self.S
        S_.op("act", lambda e: e.activation(out=junk, in_=ht, func=AF.Square, accum_out=ss), reads=[rht], writes=[rjunk, rss])
        S_.op("act", lambda e: e.activation(out=ss, in_=ss, func=AF.Sqrt, scale=1.0 / D, bias=self.epsc[0]), reads=[rss, self.epsc[1]],
              writes=[rss])
        S_.op("dve", lambda e: e.reciprocal(out=rs, in_=ss), reads=[rss], writes=[rrs])

    def eps_const(self):
        if getattr(self, "epsc", None) is None:
            ap, r = self.tile([128, 1], F32, "eps", pers=True)
            self.S.op("pool", lambda e: e.memset(ap, EPS), writes=[r])
            self.epsc = (ap, r)

    def ph_proj(self, l):
        S_, T, PS, rPS = self.S, self.T, self.PS, self.rPS
        self.eps_const()
        idb, ridb = self.C["ident_b"]
        wbf, rwbf = self.tile([128, 8, 3584], BF16, "wbf")
        wsrc = T["w_in_p"][l].rearrange("(kt p) c -> p kt c", p=128)
        for kt in range(8):
            for hf in range(2):
                self.S.dma("pool", wbf[:, kt, hf * 1792:(hf + 1) * 1792], wsrc[:, kt, hf * 1792:(hf + 1) * 1792],
                           reads=[self.rIN], uw=[rwbf])
        src, rsrc = (T["xc"], self.rIN) if l == 0 else (self.H1, self.rH1)
        nht = self.rot(2, [128, D], F32, "ht")
        junk, rjunk = self.tile([128, D], F32, "junk")
        nss = self.rot(2, [128, 1], F32, "ss")
        nrs = self.rot(2, [128, 1], F32, "rs")
        nx1 = self.rot(2, [128, D], F32, "xm1")
        nxb = self.rot(2, [128, D], BF16, "xmb")
        nxT = self.rot(2, [128, 8, 512], BF16, "xT")
        nst = self.rot(4, [128, 512], F32, "st")
        nsb = self.rot(2, [128, 640], BF16, "sb")
        PSb = PS[7].bitcast(BF16)
        pcnt = [0]
        ecnt = [0]

        def evac(dst, rdst, src_ps, rsrc_ps):
            ecnt[0] += 1
            if ecnt[0] % 2:
                S_.op("act", lambda e: e.copy(out=dst, in_=src_ps), reads=[rsrc_ps], writes=[rdst])
            else:
                S_.op("dve", lambda e: e.tensor_copy(out=dst, in_=src_ps), reads=[rsrc_ps], writes=[rdst])

        for blk in range(9):
            tok0 = blk * 512
            ntile = 4 if blk < 8 else 2
            s = 0 if blk < 8 else 1
            N = ntile * 128
            xT, rxT = nxT()
            for ti in range(ntile):
                r0 = tok0 + ti * 128
                ht, rht = nht()
                self.load(ht, rht, src[r0:r0 + 128, :], rsrc)
                ss, rss = nss()
                rs, rrs = nrs()
                self.rstd_of(ht, rht, junk, rjunk, ss, rss, rs, rrs)
                x1, rx1 = nx1()
                S_.op("dve", lambda e, ht=ht, rs=rs, x1=x1, s=s: e.scalar_tensor_tensor(
                    out=x1, in0=ht, scalar=rs, in1=self.gs[s][0], op0=ALU.mult, op1=ALU.mult),
                    reads=[rht, rrs, self.gs[s][1]], writes=[rx1])
                xb, rxb = nxb()
                S_.op("pool", lambda e, x1=x1, xb=xb, s=s: e.tensor_tensor(out=xb, in0=x1, in1=self.sh[s][0], op=ALU.add),
                      reads=[rx1, self.sh[s][1]], writes=[rxb])
                for dt in range(8):
                    S_.op("pe", lambda e, dt=dt, xb=xb: e.transpose(out=PSb[:, dt * 128:(dt + 1) * 128],
                                                                   in_=xb[:, dt * 128:(dt + 1) * 128], identity=idb),
                          reads=[rxb, ridb], writes=[rPS[7]])
                evac(xT[:, :, ti * 128:(ti + 1) * 128], rxT, PSb.rearrange("p (a b) -> p a b", a=8), rPS[7])
            for ct in range(NF // 128):
                pi = pcnt[0] % 4
                pcnt[0] += 1
                for dt in range(8):
                    S_.op("pe", lambda e, dt=dt, ct=ct, pi=pi, xT=xT, N=N: e.matmul(
                        PS[pi][:, :N], lhsT=wbf[:, dt, ct * 128:(ct + 1) * 128], rhs=xT[:, dt, :N], start=(dt == 0), stop=(dt == 7)),
                        reads=[rwbf, rxT], writes=[rPS[pi]])
                st, rst = nst()
                evac(st[:, :N], rst, PS[pi][:, :N], rPS[pi])
                self.store(self.PJF[ct * 128:(ct + 1) * 128, tok0:tok0 + N], self.rPJF, st[:, :N], rst)
            for ti in range(ntile):
                r0 = tok0 + ti * 128
                for (c0, cn, kind) in [(NF, 512, 0), (NF + 512, 512, 1), (NF + 1024, 128, 2)]:
                    pi = pcnt[0] % 4
                    pcnt[0] += 1
                    for dt in range(8):
                        S_.op("pe", lambda e, dt=dt, pi=pi, xT=xT, ti=ti, c0=c0, cn=cn: e.matmul(
                            PS[pi][:, :cn], lhsT=xT[:, dt, ti * 128:(ti + 1) * 128], rhs=wbf[:, dt, c0:c0 + cn],
                            start=(dt == 0), stop=(dt == 7)), reads=[rwbf, rxT], writes=[rPS[pi]])
                    if kind == 0:
                        st, rst = nst()
                        evac(st, rst, PS[pi], rPS[pi])
                        self.store(self.PJTf[r0:r0 + 128, :], self.rPJTf, st, rst)
                    elif kind == 1:
                        sb, rsb = nsb()
                        evac(sb[:, 0:512], rsb, PS[pi], rPS[pi])
                    else:
                        evac(sb[:, 512:640], rsb, PS[pi][:, :128], rPS[pi])
                        self.store(self.PJTb[r0:r0 + 128, :], self.rPJTb, sb, rsb)

    def ph_conv(self, l):
        S_, T = self.S, self.T
        cw, rcw = self.tile([128, 2, 3], F32, "cw")
        self.load(cw, rcw, T["conv_w_c"][l])
        seqs = [(0, S)] + ([(S, L)] if l == 0 else [])
        CH = 2048
        for (t0, n) in seqs:
            for ct in range(2):
                for c0 in range(0, n, CH):
                    m = min(CH, n - c0)
                    a = t0 + c0
                    lo = 1 if c0 > 0 else 0
                    hi = 1 if c0 + m < n else 0
                    cc, rcc = self.tile([128, CH + 2], F32, "cc")
                    cv_, rcv = self.tile([128, CH + 2], F32, "cvv")
                    cb, rcb = self.tile([128, CH], F32, "cb")
                    cz, rcz = self.tile([128, CH], F32, "cz")
                    self.load(cc[:, 1 - lo:1 + m + hi], rcc, self.PJF[F_CC + ct * 128:F_CC + (ct + 1) * 128, a - lo:a + m + hi], self.rPJF)
                    self.load(cv_[:, 1 - lo:1 + m + hi], rcv, self.PJF[F_CV + ct * 128:F_CV + (ct + 1) * 128, a - lo:a + m + hi], self.rPJF)
                    self.load(cb[:, :m], rcb, self.PJF[F_CB + ct * 128:F_CB + (ct + 1) * 128, a:a + m], self.rPJF)
                    self.load(cz[:, :m], rcz, self.PJF[F_CZ + ct * 128:F_CZ + (ct + 1) * 128, a:a + m], self.rPJF)
                    p, rp = self.tile([128, CH + 2], F32, "p")
                    S_.op("pool", lambda e, p=p: e.memset(p, 0.0), writes=[rp])
                    S_.op("dve", lambda e, p=p, cc=cc, cv_=cv_, lo=lo, hi=hi, m=m: e.tensor_tensor(
                        out=p[:, 1 - lo:1 + m + hi], in0=cc[:, 1 - lo:1 + m + hi], in1=cv_[:, 1 - lo:1 + m + hi], op=ALU.mult),
                        reads=[rcc, rcv], writes=[rp])
                    acc, racc = self.tile([128, CH], F32, "acc")
                    S_.op("dve", lambda e, p=p, acc=acc, m=m, ct=ct, cw=cw: e.tensor_scalar(
                        out=acc[:, :m], in0=p[:, 0:m], scalar1=cw[:, ct, 0:1], scalar2=None, op0=ALU.mult),
                        reads=[rp, rcw], writes=[racc])
                    for k in (1, 2):
                        S_.op("dve", lambda e, p=p, acc=acc, m=m, ct=ct, k=k, cw=cw: e.scalar_tensor_tensor(
                            out=acc[:, :m], in0=p[:, k:k + m], scalar=cw[:, ct, k:k + 1], in1=acc[:, :m], op0=ALU.mult, op1=ALU.add),
                            reads=[rp, rcw, racc], writes=[racc])
                    S_.op("act", lambda e, cz=cz, m=m: e.activation(out=cz[:, :m], in_=cz[:, :m], func=AF.Silu), reads=[rcz], writes=[rcz])
                    S_.op("pool", lambda e, acc=acc, cb=cb, m=m: e.tensor_tensor(out=cb[:, :m], in0=acc[:, :m], in1=cb[:, :m], op=ALU.mult),
                          reads=[racc, rcb], writes=[rcb])
                    res, rres = self.tile([128, CH], BF16, "res")
                    S_.op("dve", lambda e, res=res, cb=cb, cz=cz, m=m: e.tensor_tensor(out=res[:, :m], in0=cb[:, :m], in1=cz[:, :m], op=ALU.mult),
                          reads=[rcb, rcz], writes=[rres])
                    self.store(self.MIX[ct * 128:(ct + 1) * 128, a:a + m], self.rMIX, res[:, :m], rres)
                    self.S.barrier()
                    self.ar.reset()
                    cw, rcw = self.tile([128, 2, 3], F32, "cw")
                    self.load(cw, rcw, T["conv_w_c"][l])

    def ph_att(self, l):
        S_, T, PS, rPS = self.S, self.T, self.PS, self.rPS
        self.eps_const()
        blk_, rblk = self.C["blkavg"]
        rp_, rrp = self.C["ropeP"]
        onesf, ronesf = self.C["onesf"]
        g2, rg2 = self.tile([128, 2], F32, "qkg")
        self.load(g2, rg2, T["qk_g_c"][l])
        cosT, rcos = self.tile([128, TT], F32, "cosT")
        sinT, rsin = self.tile([128, TT], F32, "sinT")
        self.load(cosT, rcos, T["cosT"])
        self.load(sinT, rsin, T["sinT"])
        qT, rqT = self.tile([128, 2, TT], BF16, "qT")
        KD, rKD = self.tile([128, 2, TT], BF16, "KD")
        VA, rVA = self.tile([128, NT, 2, 65], BF16, "VA")
        S_.op("pool", lambda e: e.memset(VA, 1.0), writes=[rVA])
        for g in range(2):
            self.S.dma("sp", VA[:, :, g, 0:64], self.PJTb[:, g * 64:(g + 1) * 64].rearrange("(n p) d -> p n d", p=128),
                       reads=[self.rPJTb], uw=[rVA])
        nx = self.rot(2, [128, 512], F32, "x")
        nsq = self.rot(2, [128, 512], F32, "sq")
        nri = self.rot(2, [128, 512], F32, "ri")
        nxh = self.rot(2, [128, 512], F32, "xh")
        nt1 = self.rot(2, [128, 512], F32, "t1")
        nt2 = self.rot(2, [128, 512], F32, "t2")
        for kind in range(4):
            for blk in range(9):
                tok0 = blk * 512
                N = 512 if blk < 8 else 256
                x, rx = nx()
                if kind < 2:
                    self.load(x[:, :N], rx, self.PJF[F_Q + kind * 128:F_Q + (kind + 1) * 128, tok0:tok0 + N], self.rPJF)
                    gcol = g2[:, 0:1]
                    dst = qT[:, kind, tok0:tok0 + N]
                    rdst = rqT
                else:
                    g = kind - 2
                    for j in range(2):
                        self.S.dma("sp", x[j * 64:(j + 1) * 64, :N], self.PJF[F_K + g * 64:F_K + (g + 1) * 64, tok0:tok0 + N],
                                   reads=[self.rPJF], uw=[rx])
                    gcol = g2[:, 1:2]
                    dst = KD[:, g, tok0:tok0 + N]
                    rdst = rKD
                sq, rsq = nsq()
                S_.op("pool", lambda e, x=x, sq=sq, N=N: e.tensor_tensor(out=sq[:, :N], in0=x[:, :N], in1=x[:, :N], op=ALU.mult),
                      reads=[rx], writes=[rsq])
                S_.op("pe", lambda e, sq=sq, N=N: e.matmul(PS[0][:, :N], lhsT=blk_, rhs=sq[:, :N], start=True, stop=True),
                      reads=[rsq, rblk], writes=[rPS[0]])
                ri, rri = nri()
                S_.op("act", lambda e, ri=ri, N=N: e.activation(out=ri[:, :N], in_=PS[0][:, :N], func=AF.Sqrt, bias=self.epsc[0]),
                      reads=[rPS[0], self.epsc[1]], writes=[rri])
                S_.op("dve", lambda e, ri=ri, N=N: e.reciprocal(out=ri[:, :N], in_=ri[:, :N]), reads=[rri], writes=[rri])
                xh, rxh = nxh()
                S_.op("dve", lambda e, x=x, xh=xh, ri=ri, N=N, gcol=gcol: e.scalar_tensor_tensor(
                    out=xh[:, :N], in0=x[:, :N], scalar=gcol, in1=ri[:, :N], op0=ALU.mult, op1=ALU.mult),
                    reads=[rx, rri, rg2], writes=[rxh])
                S_.op("pe", lambda e, xh=xh, N=N: e.matmul(PS[1][:, :N], lhsT=rp_, rhs=xh[:, :N], start=True, stop=True),
                      reads=[rxh, rrp], writes=[rPS[1]])
                t1, rt1 = nt1()
                S_.op("pool", lambda e, xh=xh, t1=t1, N=N, tok0=tok0: e.tensor_tensor(
                    out=t1[:, :N], in0=xh[:, :N], in1=cosT[:, tok0:tok0 + N], op=ALU.mult), reads=[rxh, rcos], writes=[rt1])
                t2, rt2 = nt2()
                S_.op("dve", lambda e, t2=t2, N=N, tok0=tok0: e.tensor_tensor(
                    out=t2[:, :N], in0=PS[1][:, :N], in1=sinT[:, tok0:tok0 + N], op=ALU.mult), reads=[rPS[1], rsin], writes=[rt2])
                S_.op("pool", lambda e, t1=t1, t2=t2, N=N, dst=dst: e.tensor_tensor(out=dst, in0=t1[:, :N], in1=t2[:, :N], op=ALU.add),
                      reads=[rt1, rt2], uw=[rdst])
        nPT = self.rot(4, [128, 512], BF16, "PT")
        ndr = self.rot(2, [128, 512], F32, "dr")
        nbc = self.rot(2, [64, 512], F32, "bc")
        naz = self.rot(2, [64, 512], F32, "az")
        no1 = self.rot(2, [64, 512], F32, "o1")
        nres = self.rot(2, [64, 512], BF16, "ares")
        qblocks = [(b * 512, 512, list(range(NT))) for b in range(8)]
        if l == 0:
            qblocks.append((S, L, [32, 33]))
        sti = [0]
        oti = [0]
        for g in range(2):
            for (tok0, N, keys) in qblocks:
                for j in range(2):
                    h = 2 * g + j
                    oi = 4 + (oti[0] % 2)
                    oti[0] += 1
                    OT, rOT = PS[oi], rPS[oi]
                    for ki, kt in enumerate(keys):
                        si = 2 * j + (sti[0] % 2)
                        sti[0] += 1
                        ST, rST = PS[si], rPS[si]
                        S_.op("pe", lambda e, ST=ST, j=j, g=g, kt=kt, tok0=tok0, N=N: e.matmul(
                            ST[:, :N], lhsT=KD[j * 64:(j + 1) * 64, g, kt * 128:(kt + 1) * 128], rhs=qT[j * 64:(j + 1) * 64, g, tok0:tok0 + N],
                            start=True, stop=True), reads=[rKD, rqT], writes=[rST])
                        PT, rPT = nPT()
                        S_.op("act", lambda e, ST=ST, PT=PT, N=N: e.activation(out=PT[:, :N], in_=ST[:, :N], func=AF.Exp, scale=0.125),
                              reads=[rST], writes=[rPT])
                        S_.op("pe", lambda e, OT=OT, PT=PT, kt=kt, g=g, N=N, ki=ki, nk=len(keys): e.matmul(
                            OT[0:65, :N], lhsT=VA[:, kt, g, :], rhs=PT[:, :N], start=(ki == 0), stop=(ki == nk - 1)),
                            reads=[rVA, rPT], writes=[rOT])
                    dr, rdr = ndr()
                    S_.op("dve", lambda e, dr=dr, OT=OT, N=N: e.reciprocal(out=dr[64:65, :N], in_=OT[64:65, :N]), reads=[rOT], writes=[rdr])
                    S_.op("pe", lambda e, dr=dr, N=N: e.matmul(PS[6][0:64, :N], lhsT=onesf[64:65, 0:64], rhs=dr[64:65, :N], start=True, stop=True),
                          reads=[rdr, ronesf], writes=[rPS[6]])
                    bc, rbc = nbc()
                    S_.op("act", lambda e, bc=bc, N=N: e.copy(out=bc[:, :N], in_=PS[6][0:64, :N]), reads=[rPS[6]], writes=[rbc])
                    o1, ro1 = no1()
                    S_.op("dve", lambda e, o1=o1, OT=OT, bc=bc, N=N: e.tensor_tensor(out=o1[:, :N], in0=OT[0:64, :N], in1=bc[:, :N], op=ALU.mult),
                          reads=[rOT, rbc], writes=[ro1])
                    az, raz = naz()
                    self.load(az[:, :N], raz, self.PJF[F_AZ + h * 64:F_AZ + (h + 1) * 64, tok0:tok0 + N], self.rPJF)
                    S_.op("act", lambda e, az=az, N=N: e.activation(out=az[:, :N], in_=az[:, :N], func=AF.Silu), reads=[raz], writes=[raz])
                    res, rres = nres()
                    S_.op("pool", lambda e, res=res, o1=o1, az=az, N=N: e.tensor_tensor(out=res[:, :N], in0=o1[:, :N], in1=az[:, :N], op=ALU.mult),
                          reads=[ro1, raz], writes=[rres])
                    self.store(self.MIX[256 + h * 64:256 + (h + 1) * 64, tok0:tok0 + N], self.rMIX, res[:, :N], rres)

    def ph_four(self, l):
        S_, T, PS, rPS = self.S, self.T, self.PS, self.rPS
        chc, rchc = self.C["ch_c"]
        chs, rchs = self.C["ch_s"]
        seqs = [(0, S, T["dft_c"], T["dft_s"], 1.0 / 512)] + ([(S, L, T["dftc_c"], T["dftc_s"], 1.0 / 128)] if l == 0 else [])
        JC = 8
        ntc = self.rot(3, [128, JC, 512], BF16, "tc")
        nts = self.rot(3, [128, JC, 512], BF16, "ts")
        nw = self.rot(4, [128, 512], BF16, "w12")
        nfz = self.rot(2, [128, 512], F32, "fz")
        nres = self.rot(2, [128, 512], BF16, "fres")
        for (t0, n, TC, TS, scl) in seqs:
            nt = n // 128
            U, rU = self.tile([128, NT, 256], BF16, "U")
            usrc = self.PJTb[t0:t0 + n, 384:640].rearrange("(n p) c -> p n c", p=128)
            for u0 in range(0, nt, 4):
                u1 = min(nt, u0 + 4)
                self.S.dma("sp", U[:, u0:u1, :], usrc[:, u0:u1, :], reads=[self.rPJTb], uw=[rU])
            tcs = TC.rearrange("(j p) c -> p j c", p=128)
            tss = TS.rearrange("(j p) c -> p j c", p=128)
            for b0 in range(0, n, 512):
                N = min(512, n - b0)
                for j0 in range(0, nt, JC):
                    jn = min(JC, nt - j0)
                    tc, rtc = ntc()
                    ts, rts = nts()
                    self.load(tc[:, :jn, :N], rtc, tcs[:, j0:j0 + jn, b0:b0 + N])
                    self.load(ts[:, :jn, :N], rts, tss[:, j0:j0 + jn, b0:b0 + N])
                    for jj in range(jn):
                        j = j0 + jj
                        for half in range(2):
                            for w, (tb, rtb) in enumerate(((tc, rtc), (ts, rts))):
                                pi = half * 2 + w
                                S_.op("pe", lambda e, pi=pi, j=j, jj=jj, half=half, tb=tb, N=N, nt=nt, U=U: e.matmul(
                                    PS[pi][:, :N], lhsT=U[:, j, half * 128:(half + 1) * 128], rhs=tb[:, jj, :N],
                                    start=(j == 0), stop=(j == nt - 1)), reads=[rU, rtb], writes=[rPS[pi]])
                for half in range(2):
                    w1, rw1 = nw()
                    w2, rw2 = nw()
                    S_.op("act", lambda e, w1=w1, half=half, N=N: e.copy(out=w1[:, :N], in_=PS[half * 2][:, :N]), reads=[rPS[half * 2]], writes=[rw1])
                    S_.op("dve", lambda e, w2=w2, half=half, N=N: e.tensor_copy(out=w2[:, :N], in_=PS[half * 2 + 1][:, :N]),
                          reads=[rPS[half * 2 + 1]], writes=[rw2])
                    yi = 4 + half
                    S_.op("pe", lambda e, yi=yi, w1=w1, N=N: e.matmul(PS[yi][:, :N], lhsT=chc, rhs=w1[:, :N], start=True, stop=False),
                          reads=[rw1, rchc], writes=[rPS[yi]])
                    S_.op("pe", lambda e, yi=yi, w2=w2, N=N: e.matmul(PS[yi][:, :N], lhsT=chs, rhs=w2[:, :N], start=False, stop=True),
                          reads=[rw2, rchs], writes=[rPS[yi]])
                    fz, rfz = nfz()
                    self.load(fz[:, :N], rfz, self.PJF[F_FZ + half * 128:F_FZ + (half + 1) * 128, t0 + b0:t0 + b0 + N], self.rPJF)
                    S_.op("act", lambda e, fz=fz, N=N: e.activation(out=fz[:, :N], in_=fz[:, :N], func=AF.Silu), reads=[rfz], writes=[rfz])
                    res, rres = nres()
                    S_.op("dve", lambda e, res=res, yi=yi, fz=fz, N=N, scl=scl: e.scalar_tensor_tensor(
                        out=res[:, :N], in0=PS[yi][:, :N], scalar=scl, in1=fz[:, :N], op0=ALU.mult, op1=ALU.mult),
                        reads=[rPS[yi], rfz], writes=[rres])
                    self.store(self.MIX[768 + half * 128:768 + (half + 1) * 128, t0 + b0:t0 + b0 + N], self.rMIX, res[:, :N], rres)

    def ph_out(self, l):
        S_, T, PS, rPS = self.S, self.T, self.PS, self.rPS
        self.eps_const()
        wob, rwob = self.tile([128, 8, D], BF16, "wob")
        wsrc = T["w_out"][l].rearrange("(kt p) c -> p kt c", p=128)
        for kt in range(8):
            self.S.dma("pool", wob[:, kt, :], wsrc[:, kt, :], reads=[self.rIN], uw=[rwob])
        src, rsrc = (T["xc"], self.rIN) if l == 0 else (self.H1, self.rH1)
        last = (l == DEPTH - 1)
        if last:
            fg, rfg = self.tile([128, D], F32, "fg")
            self.load(fg, rfg, T["final_g_b"])
            junk, rjunk = self.tile([128, D], F32, "junk")
            nss = self.rot(2, [128, 1], F32, "ss")
            nrs = self.rot(2, [128, 1], F32, "rs")
            nob = self.rot(2, [128, D], F32, "ob")
        nmx = self.rot(2, [128, 8, 512], BF16, "mx")
        nht = self.rot(2, [128, D], F32, "ht")
        ntm = self.rot(2, [128, D], F32, "tm")
        nhn = self.rot(2, [128, D], F32, "hn")
        nblk = 8 if last else 9
        pc = [0]
        for blk in range(nblk):
            tok0 = blk * 512
            ntile = 4 if blk < 8 else 2
            s = 0 if blk < 8 else 1
            N = ntile * 128
            mx, rmx = nmx()
            self.load(mx[:, :, :N], rmx, self.MIX[:, tok0:tok0 + N].rearrange("(ct p) t -> p ct t", p=128), self.rMIX)
            for ti in range(ntile):
                r0 = tok0 + ti * 128
                ht, rht = nht()
                self.load(ht, rht, src[r0:r0 + 128, :], rsrc)
                tm, rtm = ntm()
                hn, rhn = nhn()
                for nb in range(2):
                    pi = pc[0] % 4
                    pc[0] += 1
                    for ct in range(8):
                        S_.op("pe", lambda e, pi=pi, ct=ct, mx=mx, ti=ti, nb=nb: e.matmul(
                            PS[pi], lhsT=mx[:, ct, ti * 128:(ti + 1) * 128], rhs=wob[:, ct, nb * 512:(nb + 1) * 512],
                            start=(ct == 0), stop=(ct == 7)), reads=[rmx, rwob], writes=[rPS[pi]])
                    S_.op("dve", lambda e, pi=pi, tm=tm, nb=nb, s=s: e.tensor_tensor(
                        out=tm[:, nb * 512:(nb + 1) * 512], in0=PS[pi], in1=self.gt[s][0][:, nb * 512:(nb + 1) * 512], op=ALU.mult),
                        reads=[rPS[pi], self.gt[s][1]], uw=[rtm])
                S_.op("pool", lambda e, tm=tm, hn=hn, ht=ht: e.tensor_tensor(out=hn, in0=tm, in1=ht, op=ALU.add),
                      reads=[rtm, rht], writes=[rhn])
                if not last:
                    self.store(self.H1[r0:r0 + 128, :], self.rH1, hn, rhn)
                else:
                    ss, rss = nss()
                    rs, rrs = nrs()
                    self.rstd_of(hn, rhn, junk, rjunk, ss, rss, rs, rrs)
                    ob, rob = nob()
                    S_.op("dve", lambda e, hn=hn, rs=rs, ob=ob: e.scalar_tensor_tensor(
                        out=ob, in0=hn, scalar=rs, in1=fg, op0=ALU.mult, op1=ALU.mult), reads=[rhn, rrs, rfg], writes=[rob])
                    self.store(self.out[r0:r0 + 128, :], self.rOUT, ob, rob)

    @staticmethod
    def _slot(d, i, cl):
        if i >= 32:
            cc = (i - 32) * 2 + cl
            return 1 + cc if d == 0 else 4 - cc
        c = 2 * i + cl
        return 5 + c if d == 0 else 68 - c

    def _gates(self, fx, rfx, W, lb, oml, rlb, nsg, ntmp, nlf, nkk):
        S_ = self.S
        sg, rsg = nsg()
        S_.op("act", lambda e: e.activation(out=sg[:, :W], in_=fx[:, :W], func=AF.Sigmoid), reads=[rfx], writes=[rsg])
        tmp, rtmp = ntmp()
        S_.op("dve", lambda e: e.tensor_tensor(out=tmp[:, :W], in0=sg[:, :W], in1=oml, op=ALU.mult), reads=[rsg, rlb], writes=[rtmp])
        S_.op("pool", lambda e: e.tensor_tensor(out=sg[:, :W], in0=tmp[:, :W], in1=lb, op=ALU.add), reads=[rtmp, rlb], writes=[rsg])
        lf, rlf = nlf()
        S_.op("act", lambda e: e.activation(out=lf[:, :W], in_=sg[:, :W], func=AF.Ln), reads=[rsg], writes=[rlf])
        kk, rkk = nkk()
        S_.op("pool", lambda e: e.tensor_tensor(out=kk[:, :W], in0=oml, in1=tmp[:, :W], op=ALU.subtract), reads=[rtmp, rlb], writes=[rkk])
        return lf, rlf, kk, rkk

    def ph_hg(self, l):
        S_, T, PS, rPS = self.S, self.T, self.PS, self.rPS
        self.eps_const()
        dm, rdm = self.C["hg_dm"]
        am, ram = self.C["hg_am"]
        ind, rind = self.C["hg_ind"]
        mask, rmask = self.C["hg_mask"]
        idf, ridf = self.C["ident_f"]
        ones64, rones64 = self.C["ones64"]
        lbt, rlb = self.tile([128, 2, 256], F32, "lbt")
        oml, _ = self.tile([128, 2, 256], F32, "oml")
        if l == 0:
            S_.op("pool", lambda e: e.memset(lbt, 0.0), writes=[rlb])
            S_.op("pool", lambda e: e.memset(oml, 1.0), writes=[rlb])
        else:
            lbp, rlbp = self.tile([128, 2, 2, 256], F32, "lbp")
            self.load(lbp, rlbp, T["lbp_b"])
            S_.op("act", lambda e: e.activation(out=lbp, in_=lbp, func=AF.Exp), reads=[rlbp], writes=[rlbp])
            den, rden = self.tile([128, 2, 256], F32, "den")
            S_.op("dve", lambda e: e.tensor_tensor(out=den, in0=lbp[:, :, 0, :], in1=lbp[:, :, 1, :], op=ALU.add), reads=[rlbp], writes=[rden])
            S_.op("dve", lambda e: e.reciprocal(out=den, in_=den), reads=[rden], writes=[rden])
            S_.op("dve", lambda e: e.tensor_tensor(out=lbt, in0=lbp[:, :, 1, :], in1=den, op=ALU.mult), reads=[rlbp, rden], writes=[rlb])
            S_.op("dve", lambda e: e.tensor_tensor(out=oml, in0=lbp[:, :, 0, :], in1=den, op=ALU.mult), reads=[rlbp, rden], writes=[rlb])
        lbf = lbt.rearrange("p a c -> p (a c)")
        omf = oml.rearrange("p a c -> p (a c)")
        hgg, rhgg = self.tile([64, 4], F32, "hgg")
        self.load(hgg, rhgg, T["hg_g_c"][l])
        SR = [self.tile([128, 2, 68, 64], BF16, "SR") for _ in range(2)]
        nfx = self.rot(2, [128, 512], F32, "fx")
        nsg = self.rot(2, [128, 512], F32, "sg")
        ntmp = self.rot(2, [128, 512], F32, "tmp")
        nlf = self.rot(2, [128, 512], F32, "lf")
        nkk = self.rot(2, [128, 512], F32, "kk")
        nV = self.rot(2, [128, 256], BF16, "V")
        mark = self.ar.cur
        UPD, rUPD = self.tile([128, 2, 64, 69], F32, "UPD")
        Sst, rSst = self.tile([128, 2, 64, 69], F32, "Sst")
        ER, rER = self.tile([128, 2, 69], F32, "ER")
        ASUM, rASUM = self.tile([128, 2, 69], F32, "ASUM")
        RSUM, rRSUM = self.tile([128, 2, 69], F32, "RSUM")
        EA, rEA = self.tile([128, 2, 69], F32, "EA")
        nE = self.rot(2, [128, 256], F32, "E")
        nkl = self.rot(2, [128, 256], BF16, "kl")
        cnt = [0]

        def cp(out, rout, in_, rin, uw=False):
            cnt[0] += 1
            kw = dict(reads=[rin], uw=[rout]) if uw else dict(reads=[rin], writes=[rout])
            if cnt[0] % 2:
                S_.op("act", lambda e: e.copy(out=out, in_=in_), **kw)
            else:
                S_.op("dve", lambda e: e.tensor_copy(out=out, in_=in_), **kw)

        for d in range(2):
            S_.op("pool", lambda e: e.memset(UPD, 0.0), writes=[rUPD])
            S_.op("pool", lambda e: e.memset(ASUM, 0.0), writes=[rASUM])
            S_.op("pool", lambda e: e.memset(RSUM, 0.0), writes=[rRSUM])
            for i in range(NT):
                r0 = i * 128
                fx, rfx = nfx()
                self.load(fx[:, :256], rfx, self.PJTf[r0:r0 + 128, d * 256:(d + 1) * 256], self.rPJTf)
                V, rV = nV()
                self.load(V, rV, self.PJTb[r0:r0 + 128, 128:384], self.rPJTb)
                lf, rlf, kk, rkk = self._gates(fx, rfx, 256, lbf[:, d * 256:(d + 1) * 256], omf[:, d * 256:(d + 1) * 256], rlb,
                                               nsg, ntmp, nlf, nkk)
                S_.op("pe", lambda e, lf=lf, d=d: e.matmul(PS[0][:, :256], lhsT=dm[:, d, :], rhs=lf[:, :256], start=True, stop=True),
                      reads=[rlf, rdm], writes=[rPS[0]])
                E, rE = nE()
                S_.op("act", lambda e, E=E: e.activation(out=E, in_=PS[0][:, :256], func=AF.Exp), reads=[rPS[0]], writes=[rE])
                kl, rkl = nkl()
                S_.op("dve", lambda e, kl=kl, kk=kk, E=E: e.tensor_tensor(out=kl, in0=kk[:, :256], in1=E, op=ALU.mult),
                      reads=[rkk, rE], writes=[rkl])
                for half in range(2):
                    S_.op("pe", lambda e, lf=lf, half=half, d=d: e.matmul(
                        PS[1][:, half * 32:(half + 1) * 32], lhsT=lf[:, half * 128:(half + 1) * 128], rhs=ind[:, d, :], start=True, stop=True),
                        reads=[rlf, rind], writes=[rPS[1]])
                p1 = PS[1][:, 0:64].rearrange("p (a b) -> p a b", a=2)
                for cl in range(2):
                    sl = self._slot(d, i, cl)
                    cp(ASUM[:, :, sl:sl + 1], rASUM, p1[:, :, cl:cl + 1], rPS[1], uw=True)
                    cp(RSUM[:, :, sl:sl + 1], rRSUM, p1[:, :, 2 + cl:3 + cl], rPS[1], uw=True)
                for cl in range(2):
                    for pair in range(2):
                        S_.op("pe", lambda e, kl=kl, V=V, cl=cl, pair=pair: e.matmul(
                            PS[2 + cl][:, pair * 256:(pair + 1) * 256], lhsT=kl[cl * 64:(cl + 1) * 64, pair * 128:(pair + 1) * 128],
                            rhs=V[cl * 64:(cl + 1) * 64, :], start=True, stop=True), reads=[rkl, rV], writes=[rPS[2 + cl]])
                for cl in range(2):
                    sl = self._slot(d, i, cl)
                    for pair in range(2):
                        for hl in range(2):
                            c0 = pair * 256 + (pair * 2 + hl) * 64
                            cp(UPD[hl * 64:(hl + 1) * 64, pair, :, sl:sl + 1],
                               rUPD, PS[2 + cl][hl * 64:(hl + 1) * 64, c0:c0 + 64].unsqueeze(2), rPS[2 + cl], uw=True)
            S_.op("act", lambda e: e.activation(out=EA, in_=ASUM, func=AF.Exp), reads=[rASUM], writes=[rEA])
            S_.op("pool", lambda e: e.memset(Sst, 0.0), writes=[rSst])
            for sl in range(1, 69):
                for pair in range(2):
                    S_.op("dve", lambda e, sl=sl, pair=pair: e.scalar_tensor_tensor(
                        out=Sst[:, pair, :, sl:sl + 1], in0=Sst[:, pair, :, sl - 1:sl], scalar=EA[:, pair, sl:sl + 1],
                        in1=UPD[:, pair, :, sl:sl + 1], op0=ALU.mult, op1=ALU.add), reads=[rSst, rEA, rUPD], writes=[rSst])
            S_.op("act", lambda e: e.activation(out=ER, in_=RSUM, func=AF.Exp), reads=[rRSUM], writes=[rER])
            for pair in range(2):
                S_.op("dve", lambda e, pair=pair, d=d: e.tensor_tensor(
                    out=SR[d][0][:, pair], in0=Sst[:, pair, :, 0:68].rearrange("p v s -> p s v"),
                    in1=ER[:, pair, 1:69].unsqueeze(2).broadcast_to([128, 68, 64]), op=ALU.mult),
                    reads=[rSst, rER], writes=[SR[d][1]])
        self.S.barrier()
        self.ar.cur = mark
        nqh = self.rot(2, [128, 2, 128], F32, "qh")
        nhz = self.rot(2, [64, 4, 128], F32, "hz")
        nEQ = self.rot(2, [128, 512], F32, "EQ")
        nEK = self.rot(2, [128, 512], F32, "EK")
        nqa = self.rot(2, [128, 4, 128], BF16, "qa")
        nkb = self.rot(2, [128, 4, 128], BF16, "kb")
        nkz = self.rot(2, [128, 4, 128], BF16, "kz")
        nqz = self.rot(4, [128, 4, 128], BF16, "qz")
        cm, rcm = self.C["hg_cm"]
        rowm, rrowm = self.C["hg_rowm"]
        mask4 = mask.rearrange("p (d a t) -> p d a t", d=2, a=2)
        SCB = [[2, 3], [6, 7]]
        scs = [[self.tile([128, 512], BF16, "sc") for _ in range(2)] for _ in range(2)]
        for a_ in scs:
            for (ap, r) in a_:
                S_.op("pool", lambda e, ap=ap: e.memset(ap, 0.0), writes=[r])
        nsq = self.rot(2, [64, 512], F32, "hsq")
        nri = self.rot(2, [64, 512], F32, "hri")
        no1 = self.rot(2, [64, 512], F32, "ho1")
        nres = self.rot(2, [64, 4, 128], BF16, "hres")
        tiles = list(range(32)) + ([32, 33] if l == 0 else [])
        for n_, i in enumerate(tiles):
            r0 = i * 128
            fx, rfx = nfx()
            self.load(fx, rfx, self.PJTf[r0:r0 + 128, :], self.rPJTf)
            V, rV = nV()
            self.load(V, rV, self.PJTb[r0:r0 + 128, 128:384], self.rPJTb)
            qh, rqh = nqh()
            self.load(qh, rqh, self.PJF[F_HQ:F_HQ + 256, r0:r0 + 128].rearrange("(a p) t -> p a t", p=128), self.rPJF)
            S_.op("act", lambda e, qh=qh: e.activation(out=qh, in_=qh, func=AF.Silu), reads=[rqh], writes=[rqh])
            hz, rhz = nhz()
            self.load(hz, rhz, self.PJF[F_HZ:F_HZ + 256, r0:r0 + 128].rearrange("(h v) t -> v h t", v=64), self.rPJF)
            S_.op("act", lambda e, hz=hz: e.activation(out=hz, in_=hz, func=AF.Silu), reads=[rhz], writes=[rhz])
            lf, rlf, kk, rkk = self._gates(fx, rfx, 512, lbf, omf, rlb, nsg, ntmp, nlf, nkk)
            for d in range(2):
                for half in range(2):
                    c0 = (d * 2 + half) * 128
                    S_.op("pe", lambda e, lf=lf, c0=c0, d=d: e.matmul(PS[0][:, c0:c0 + 128], lhsT=lf[:, c0:c0 + 128], rhs=am[:, d, :],
                                                                     start=True, stop=True), reads=[rlf, ram], writes=[rPS[0]])
                    S_.op("pe", lambda e, kk=kk, c0=c0: e.matmul(PS[1][:, c0:c0 + 128], lhsT=kk[:, c0:c0 + 128], rhs=idf,
                                                                 start=True, stop=True), reads=[rkk, ridf], writes=[rPS[1]])
            EQ, rEQ = nEQ()
            EK, rEK = nEK()
            S_.op("act", lambda e, EQ=EQ: e.activation(out=EQ, in_=PS[0], func=AF.Exp), reads=[rPS[0]], writes=[rEQ])
            S_.op("act", lambda e, EK=EK: e.activation(out=EK, in_=PS[0], func=AF.Exp, scale=-1.0), reads=[rPS[0]], writes=[rEK])
            qa, rqa = nqa()
            kb, rkb = nkb()
            kz, rkz = nkz()
            for d in range(2):
                S_.op("pool", lambda e, qa=qa, EQ=EQ, qh=qh, d=d: e.tensor_tensor(
                    out=qa[:, d * 2:d * 2 + 2, :], in0=EQ[:, d * 256:(d + 1) * 256].rearrange("p (a b) -> p a b", a=2), in1=qh, op=ALU.mult),
                    reads=[rEQ, rqh], uw=[rqa])
            qz = []
            for hl in range(2):
                q_, rq_ = nqz()
                S_.op("pool", lambda e, q_=q_, qa=qa, hl=hl: e.tensor_scalar(
                    out=q_.rearrange("p a b -> p (a b)"), in0=qa.rearrange("p a b -> p (a b)"), scalar1=rowm[:, hl:hl + 1], scalar2=None,
                    op0=ALU.mult), reads=[rqa, rrowm], writes=[rq_])
                qz.append((q_, rq_))
            S_.op("dve", lambda e, kb=kb, EK=EK: e.tensor_tensor(out=kb.rearrange("p a b -> p (a b)"), in0=PS[1], in1=EK, op=ALU.mult),
                  reads=[rPS[1], rEK], writes=[rkb])
            S_.op("pool", lambda e, kb=kb, kz=kz: e.tensor_tensor(out=kz.rearrange("p a b -> p (a b)"), in0=kb.rearrange("p a b -> p (a b)"),
                                                                  in1=cm, op=ALU.mult), reads=[rkb, rcm], writes=[rkz])
            for d in range(2):
                for h in range(4):
                    pair, hl = h // 2, h % 2
                    dp = d * 2 + pair
                    bk = SCB[d][hl]
                    for cl in range(2):
                        c0 = pair * 128 + cl * 64
                        t0_ = cl * 64
                        ta = (32, 64) if d == 0 else (0, 32)
                        tb_ = (0, 32) if d == 0 else (32, 64)
                        S_.op("pe", lambda e, bk=bk, hl=hl, dp=dp, c0=c0, t0_=t0_, ta=ta, kb=kb, qa=qa: e.matmul(
                            PS[bk][:, c0 + ta[0]:c0 + ta[1]], lhsT=kb[hl * 64:(hl + 1) * 64, dp, :],
                            rhs=qa[hl * 64:(hl + 1) * 64, dp, t0_ + ta[0]:t0_ + ta[1]], start=True, stop=True),
                            reads=[rkb, rqa], writes=[rPS[bk]])
                        S_.op("pe", lambda e, bk=bk, hl=hl, dp=dp, c0=c0, t0_=t0_, tb_=tb_, kz=kz, qa=qa: e.matmul(
                            PS[bk][:, c0 + tb_[0]:c0 + tb_[1]], lhsT=kz[hl * 64:(hl + 1) * 64, dp, :],
                            rhs=qa[hl * 64:(hl + 1) * 64, dp, t0_ + tb_[0]:t0_ + tb_[1]], start=True, stop=True),
                            reads=[rkz, rqa], writes=[rPS[bk]])
            sc = scs[n_ % 2]
            for d in range(2):
                for hl in range(2):
                    bk = SCB[d][hl]
                    S_.op("dve", lambda e, d=d, hl=hl, bk=bk, sc=sc: e.tensor_tensor(
                        out=sc[d][0].rearrange("p (a b t) -> p a b t", a=2, b=2)[:, :, hl, :],
                        in0=PS[bk][:, 0:256].rearrange("p (a t) -> p a t", a=2), in1=mask4[:, d, :, :], op=ALU.mult),
                        reads=[rPS[bk], rmask], uw=[sc[d][1]])
            for h in range(4):
                pair, hl = h // 2, h % 2
                for d in range(2):
                    S_.op("pe", lambda e, h=h, d=d, V=V, sc=sc: e.matmul(
                        PS[4][0:64, h * 128:(h + 1) * 128], lhsT=V[:, h * 64:(h + 1) * 64], rhs=sc[d][0][:, h * 128:(h + 1) * 128],
                        start=(d == 0), stop=False), reads=[rV, sc[d][1]], writes=[rPS[4]])
                for cl in range(2):
                    for d in range(2):
                        sl = self._slot(d, i, cl)
                        S_.op("pe", lambda e, h=h, d=d, cl=cl, sl=sl, pair=pair, hl=hl, qz=qz: e.matmul(
                            PS[4][0:64, h * 128 + cl * 64:h * 128 + (cl + 1) * 64], lhsT=SR[d][0][:, pair, sl - 1, :],
                            rhs=qz[hl][0][:, d * 2 + pair, cl * 64:(cl + 1) * 64], start=False, stop=(cl == 1 and d == 1)),
                            reads=[SR[d][1], qz[hl][1]], writes=[rPS[4]])
            sq, rsq = nsq()
            S_.op("act", lambda e, sq=sq: e.activation(out=sq, in_=PS[4][0:64, :], func=AF.Square), reads=[rPS[4]], writes=[rsq])
            S_.op("pe", lambda e, sq=sq: e.matmul(PS[5][0:64, :], lhsT=ones64, rhs=sq, start=True, stop=True),
                  reads=[rsq, rones64], writes=[rPS[5]])
            ri, rri = nri()
            S_.op("act", lambda e, ri=ri: e.activation(out=ri, in_=PS[5][0:64, :], func=AF.Sqrt, bias=self.epsc[0][0:64, :]),
                  reads=[rPS[5], self.epsc[1]], writes=[rri])
            S_.op("dve", lambda e, ri=ri: e.reciprocal(out=ri, in_=ri), reads=[rri], writes=[rri])
            o1, ro1 = no1()
            S_.op("dve", lambda e, o1=o1, ri=ri: e.tensor_tensor(out=o1, in0=PS[4][0:64, :], in1=ri, op=ALU.mult),
                  reads=[rPS[4], rri], writes=[ro1])
            res, rres = nres()
            for h in range(4):
                S_.op("dve", lambda e, res=res, o1=o1, hz=hz, h=h: e.scalar_tensor_tensor(
                    out=res[:, h, :], in0=o1[:, h * 128:(h + 1) * 128], scalar=hgg[:, h:h + 1], in1=hz[:, h, :], op0=ALU.mult, op1=ALU.mult),
                    reads=[ro1, rhz, rhgg], uw=[rres])
            self.store(self.MIX[512:768, r0:r0 + 128].rearrange("(h v) t -> v h t", v=64), self.rMIX, res, rres)


_PROG = {}


def kernel(**inputs):
    maps = prepare_inputs({k: np.asarray(v) for k, v in inputs.items()})
    if "p" not in _PROG:
        _PROG["p"] = Prog()
    res = run_bass_kernel_spmd(_PROG["p"].nc, maps, core_ids=list(range(8)))
    return np.stack([np.asarray(r["out"], dtype=np.float32) for r in res.results], axis=0)
```

```python
import numpy as np
import ml_dtypes
import concourse.bass as bass
import concourse.mybir as mybir
from concourse.bass_utils import run_bass_kernel_spmd

F32 = mybir.dt.float32
BF16 = mybir.dt.bfloat16
I32 = mybir.dt.int32
AF = mybir.ActivationFunctionType
ALU = mybir.AluOpType
AX = mybir.AxisListType

D = 1024
S = 4096
L = 256
TT = S + L
NT = TT // 128
DEPTH = 2
NF = 2432
EPS = 1e-6
F_CB, F_CC, F_CV, F_CZ, F_Q, F_K, F_AZ, F_HQ, F_HZ, F_FZ = 0, 256, 512, 768, 1024, 1280, 1408, 1664, 1920, 2176
O_CB, O_CC, O_CV, O_CZ, O_Q, O_K, O_V, O_AZ, O_HQ, O_FF, O_FB, O_HI, O_HZ, O_FU, O_FZ = (
    0, 256, 512, 768, 1024, 1280, 1408, 1536, 1792, 2048, 2304, 2560, 2816, 3072, 3328)


class Res:
    __slots__ = ("w", "u", "r", "name")

    def __init__(self, name=""):
        self.w = {}
        self.u = {}
        self.r = {}
        self.name = name


class _Q:
    def __init__(self, name, sem, key):
        self.name = name
        self.sem = sem
        self.key = key
        self.count = 0
        self.ops = []
        self.known = {}


class Sched:
    def __init__(self, nc, ndma=8):
        self.nc = nc
        self.sems = {}
        self.q = {}
        for i, n in enumerate(["pe", "act", "dve", "pool", "sp"]):
            sem = nc.alloc_semaphore("q_" + n)
            self.sems[n] = sem
            self.q[n] = _Q(n, sem, n)
        self.dpool = {}
        self.di = {}
        for qn in ["sp", "pool", "act"]:
            lst = []
            for i in range(ndma):
                key = "d_%s_%d" % (qn, i)
                self.sems[key] = nc.alloc_semaphore(key)
                lst.append([key, 0])
            self.dpool[qn] = lst
            self.di[qn] = 0

    def _wait(self, q, need):
        for k, v in need.items():
            if q.known.get(k, 0) >= v:
                continue
            q.known[k] = v
            q.ops.append(("w", k, v))

    @staticmethod
    def _add(need, d):
        for k, v in d.items():
            if need.get(k, 0) < v:
                need[k] = v

    def _hazards(self, reads, writes, uw):
        need = {}
        for r in reads:
            self._add(need, r.w)
            self._add(need, r.u)
        for r in writes:
            self._add(need, r.w)
            self._add(need, r.u)
            self._add(need, r.r)
        for r in uw:
            self._add(need, r.w)
            self._add(need, r.r)
        return need

    def _commit(self, k, v, reads, writes, uw):
        for r in reads:
            if r.r.get(k, 0) < v:
                r.r[k] = v
        for r in writes:
            r.w = {k: v}
            r.u = {}
            r.r = {}
        for r in uw:
            if r.u.get(k, 0) < v:
                r.u[k] = v
            r.r = {}

    def op(self, qn, fn, reads=(), writes=(), uw=()):
        q = self.q[qn]
        need = self._hazards(reads, writes, uw)
        if qn == "pe":
            need.pop("pe", None)
        self._wait(q, need)
        q.count += 1
        q.ops.append(("i", fn))
        self._commit(q.key, q.count, reads, writes, uw)

    def dma(self, qn, out, in_, reads=(), writes=(), uw=(), **kw):
        q = self.q[qn]
        pool = self.dpool[qn]
        slot = pool[self.di[qn] % len(pool)]
        self.di[qn] += 1
        need = self._hazards(reads, writes, uw)
        if slot[1] > 0:
            self._add(need, {slot[0]: slot[1]})
        self._wait(q, need)
        slot[1] += 16
        q.ops.append(("d", out, in_, kw, slot[0]))
        self._commit(slot[0], slot[1], reads, writes, uw)

    def barrier(self):
        allev = {}
        for n, q in self.q.items():
            if q.count > 0:
                allev[n] = q.count
        for qn, pool in self.dpool.items():
            for key, val in pool:
                if val > 0:
                    allev[key] = val
        for n, q in self.q.items():
            need = dict(allev)
            need.pop(n, None)
            self._wait(q, need)

    def emit(self):
        nc = self.nc
        sems = self.sems

        def run(q, e):
            for o in q.ops:
                if o[0] == "w":
                    e.wait_ge(sems[o[1]], o[2])
                elif o[0] == "i":
                    o[1](e).then_inc(q.sem, 1)
                else:
                    e.dma_start(out=o[1], in_=o[2], **o[3]).then_inc(sems[o[4]], 16)

        with nc.Block() as block:
            @block.tensor
            def _(e):
                run(self.q["pe"], e)

            @block.scalar
            def _(e):
                run(self.q["act"], e)

            @block.vector
            def _(e):
                run(self.q["dve"], e)

            @block.gpsimd
            def _(e):
                run(self.q["pool"], e)

            @block.sync
            def _(e):
                run(self.q["sp"], e)


class Arena:
    def __init__(self, nc, lo, hi):
        self.nc = nc
        self.lo = lo
        self.hi = hi
        self.cur = lo
        self.n = 0

    def reset(self):
        self.cur = self.lo

    def alloc(self, shape, dtype, name="t"):
        free = 1
        for s in shape[1:]:
            free *= s
        esz = 4 if dtype in (F32, I32) else 2
        nbytes = free * esz
        off = (self.cur + 63) // 64 * 64
        assert off + nbytes <= self.hi, "SBUF arena overflow %s %d+%d>%d" % (name, off, nbytes, self.hi)
        self.cur = off + nbytes
        self.n += 1
        h = self.nc.alloc_sbuf_tensor_at("%s_%d" % (name, self.n), [shape[0], free], dtype, offset=off)
        ap = h.ap()
        if len(shape) == 3:
            ap = ap.rearrange("p (a b) -> p a b", a=shape[1])
        elif len(shape) == 4:
            ap = ap.rearrange("p (a b c) -> p a b c", a=shape[1], b=shape[2])
        return ap


def _bf(a):
    return np.ascontiguousarray(a.astype(ml_dtypes.bfloat16))


_CONST_CACHE = {}


def host_constants():
    if _CONST_CACHE:
        return _CONST_CACHE
    c = {}
    c["ident_f"] = np.eye(128, dtype=np.float32)
    c["ident_b"] = _bf(np.eye(128, dtype=np.float32))
    bo = np.zeros((128, 128), np.float32)
    bo[:64, :64] = 1.0 / 64
    bo[64:, 64:] = 1.0 / 64
    c["blkavg"] = bo
    pt = np.zeros((128, 128), np.float32)
    for i in range(64):
        pt[2 * i + 1, 2 * i] = -1.0
        pt[2 * i, 2 * i + 1] = 1.0
    c["ropeP"] = pt
    t = np.arange(S)
    r = (t // 64).astype(np.float32)
    cc = (t % 64).astype(np.float32)
    freqs = (10000.0 ** (-np.arange(16, dtype=np.float32) / 16)).astype(np.float32)
    ang = np.concatenate([r[:, None] * freqs, cc[:, None] * freqs], axis=-1).astype(np.float32)
    cos = np.cos(ang).astype(np.float32)
    sin = np.sin(ang).astype(np.float32)
    cosT = np.ones((128, TT), np.float32)
    sinT = np.zeros((128, TT), np.float32)
    for d in range(64):
        cosT[d, :S] = cos[:, d // 2]
        sinT[d, :S] = sin[:, d // 2]
    cosT[64:] = cosT[:64]
    sinT[64:] = sinT[:64]
    c["cosT"] = cosT
    c["sinT"] = sinT
    s_ = np.arange(128)[:, None]
    t_ = np.arange(128)[None, :]
    same = (s_ // 64) == (t_ // 64)
    MID = 31
    midt = (t_ // 64) * 64 + MID
    dm_f = (same & (s_ > t_)).astype(np.float32)
    dm_b = (same & (s_ < t_)).astype(np.float32)
    am_f = (same & (s_ <= t_)).astype(np.float32) - (same & (s_ <= midt)).astype(np.float32)
    am_b = (same & (s_ >= t_)).astype(np.float32) - (same & (s_ >= midt)).astype(np.float32)
    c["hg_dm"] = np.stack([dm_f, dm_b])
    c["hg_am"] = np.stack([am_f, am_b])
    ind = np.zeros((2, 128, 32), np.float32)
    sl = np.arange(128)
    for cl in range(2):
        inch = (sl // 64) == cl
        ind[0, :, cl] = inch
        ind[1, :, cl] = inch
        ind[0, :, 2 + cl] = inch & (sl % 64 <= MID)
        ind[1, :, 2 + cl] = inch & (sl % 64 >= MID)
    c["hg_ind"] = ind
    s6 = np.arange(64)[:, None]
    t6 = np.arange(64)[None, :]
    mk_f = (s6 <= t6).astype(np.int32)
    mk_b = (s6 >= t6).astype(np.int32)
    mk = np.zeros((128, 2, 2, 128), np.float32)
    for cl in range(2):
        for pr in range(2):
            mk[cl * 64:(cl + 1) * 64, 0, pr, cl * 64:(cl + 1) * 64] = mk_f
            mk[cl * 64:(cl + 1) * 64, 1, pr, cl * 64:(cl + 1) * 64] = mk_b
    c["hg_mask"] = np.ascontiguousarray(mk.reshape(128, 512))
    rm = np.zeros((128, 2), np.float32)
    rm[:64, 0] = 1.0
    rm[64:, 1] = 1.0
    c["hg_rowm"] = rm
    tl = np.arange(128) % 64
    cm = np.zeros((128, 4, 128), np.float32)
    cm[:, 0:2, :] = (tl < 32)[None, None, :]
    cm[:, 2:4, :] = (tl >= 32)[None, None, :]
    c["hg_cm"] = _bf(cm.reshape(128, 512))
    c["ones64"] = np.full((64, 64), 1.0 / 64, np.float32)
    tt = np.arange(S, dtype=np.int64)
    m = (tt[:, None] * tt[None, :]) % S
    ang = (2.0 * np.pi / S) * m.astype(np.float64)
    c["dft_c"] = _bf(np.cos(ang))
    c["dft_s"] = _bf(np.sin(ang))
    tl = np.arange(L, dtype=np.int64)
    m = (tl[:, None] * tl[None, :]) % L
    ang = (2.0 * np.pi / L) * m.astype(np.float64)
    c["dftc_c"] = _bf(np.cos(ang))
    c["dftc_s"] = _bf(np.sin(ang))
    d_ = np.arange(64)
    a64 = (2.0 * np.pi / 64) * ((d_[:, None] * d_[None, :]) % 64).astype(np.float64)
    cb = np.zeros((128, 128))
    sb = np.zeros((128, 128))
    for g in range(2):
        cb[g * 64:(g + 1) * 64, g * 64:(g + 1) * 64] = np.cos(a64)
        sb[g * 64:(g + 1) * 64, g * 64:(g + 1) * 64] = -np.sin(a64)
    c["ch_c"] = _bf(cb)
    c["ch_s"] = _bf(sb)
    _CONST_CACHE.update(c)
    return c


CONST_SPECS = [
    ("ident_f", [128, 128], F32), ("ident_b", [128, 128], BF16), ("blkavg", [128, 128], F32),
    ("ropeP", [128, 128], F32), ("cosT", [128, TT], F32), ("sinT", [128, TT], F32),
    ("hg_dm", [2, 128, 128], F32), ("hg_am", [2, 128, 128], F32), ("hg_ind", [2, 128, 32], F32),
    ("hg_mask", [128, 512], F32), ("hg_rowm", [128, 2], F32), ("hg_cm", [128, 512], BF16), ("ones64", [64, 64], F32),
    ("dft_c", [S, S], BF16), ("dft_s", [S, S], BF16), ("dftc_c", [L, L], BF16), ("dftc_s", [L, L], BF16),
    ("ch_c", [128, 128], BF16), ("ch_s", [128, 128], BF16),
]

IN_SPECS = [
    ("xc", [TT, D], F32),
    ("cvec", [128, 2, 8], F32),
    ("w_mod", [DEPTH, D, 3 * D], F32),
    ("b_mod_b", [DEPTH, 128, 3 * D], F32),
    ("norm_g_b", [DEPTH, 128, D], F32),
    ("w_in_p", [DEPTH, D, 3584], F32),
    ("conv_w_c", [DEPTH, 128, 2, 3], F32),
    ("qk_g_c", [DEPTH, 128, 2], F32),
    ("lbp_b", [128, 2, 2, 256], F32),
    ("hg_g_c", [DEPTH, 64, 4], F32),
    ("w_out", [DEPTH, D, D], F32),
    ("final_g_b", [128, D], F32),
]


def prepare_inputs(inp):
    perm = np.concatenate([
        np.arange(O_CB, O_CB + 1024),
        np.arange(O_Q, O_Q + 256), np.arange(O_K, O_K + 128), np.arange(O_AZ, O_AZ + 256),
        np.arange(O_HQ, O_HQ + 256), np.arange(O_HZ, O_HZ + 256), np.arange(O_FZ, O_FZ + 256),
        np.arange(O_FF, O_FF + 512),
        np.arange(O_V, O_V + 128), np.arange(O_HI, O_HI + 256), np.arange(O_FU, O_FU + 256)])
    assert perm.shape[0] == 3584
    f = np.float32
    shared = {}
    shared["w_mod"] = np.ascontiguousarray(inp["w_mod"], dtype=f)
    shared["b_mod_b"] = np.ascontiguousarray(np.broadcast_to(inp["b_mod"][:, None, :], (DEPTH, 128, 3 * D)), dtype=f)
    shared["norm_g_b"] = np.ascontiguousarray(np.broadcast_to(inp["norm_g"][:, None, :], (DEPTH, 128, D)), dtype=f)
    shared["w_in_p"] = np.ascontiguousarray(inp["w_in"][:, :, perm], dtype=f)
    cw = inp["conv_w"]
    shared["conv_w_c"] = np.ascontiguousarray(cw.reshape(DEPTH, 3, 2, 128).transpose(0, 3, 2, 1), dtype=f)
    qg = np.concatenate([inp["q_norm_g"], inp["q_norm_g"]], axis=1)
    kg = np.concatenate([inp["k_norm_g"], inp["k_norm_g"]], axis=1)
    shared["qk_g_c"] = np.ascontiguousarray(np.stack([qg, kg], axis=-1), dtype=f)
    shared["lbp_b"] = np.ascontiguousarray(np.broadcast_to(inp["hgrn_lb"][None], (128, 2, DEPTH, 256)), dtype=f)
    hg = inp["hgrn_norm_g"].reshape(DEPTH, 4, 64).transpose(0, 2, 1)
    shared["hg_g_c"] = np.ascontiguousarray(hg, dtype=f)
    shared["w_out"] = np.ascontiguousarray(inp["w_out"], dtype=f)
    shared["final_g_b"] = np.ascontiguousarray(np.broadcast_to(inp["final_g"][None, :], (128, D)), dtype=f)
    shared.update(host_constants())
    maps = []
    for b in range(8):
        m = dict(shared)
        m["xc"] = np.ascontiguousarray(np.concatenate([inp["x"][b], inp["ctx"][b]], axis=0), dtype=f)
        cv = np.stack([inp["c"][b], inp["c_ctx"]], axis=0)
        m["cvec"] = np.ascontiguousarray(cv.reshape(2, 8, 128).transpose(2, 0, 1), dtype=f)
        maps.append(m)
    return maps


P_START = 17 * 1024
P_END = 61 * 1024
A_END = 222 * 1024


class Prog:
    def __init__(self, layers=(0, 1), phases=("mod", "proj", "conv", "att", "hg", "four", "out"), dbg=()):
        self.nc = nc = bass.Bass("TRN2", target_bir_lowering=False)
        self.layers = layers
        self.phases = phases
        self.T = {}
        for name, shape, dt in IN_SPECS + CONST_SPECS:
            self.T[name] = nc.dram_tensor(name, shape, dt, kind="ExternalInput").ap()
        self.out = nc.dram_tensor("out", [S, D], F32, kind="ExternalOutput").ap()
        kind = "ExternalOutput" if dbg else "Internal"
        self.PJF = nc.dram_tensor("PJF", [NF, TT], F32, kind=kind).ap()
        self.PJTf = nc.dram_tensor("PJTf", [TT, 512], F32, kind=kind).ap()
        self.PJTb = nc.dram_tensor("PJTb", [TT, 640], BF16, kind=kind).ap()
        self.MIX = nc.dram_tensor("MIX", [D, TT], BF16, kind=kind).ap()
        self.H1 = nc.dram_tensor("H1", [TT, D], F32, kind=kind).ap()
        self.dbg = dbg
        self.rPJF, self.rPJTf, self.rPJTb, self.rMIX, self.rH1, self.rOUT = (Res(n) for n in
                                                                              ["PJF", "PJTf", "PJTb", "MIX", "H1", "OUT"])
        self.rIN = Res("in")
        self.S = Sched(nc)
        self.PS = [nc.alloc_psum_tensor("ps%d" % i, [128, 512], F32).ap() for i in range(8)]
        self.rPS = [Res("ps%d" % i) for i in range(8)]
        self.pers = Arena(nc, P_START, P_END)
        self.ar = Arena(nc, P_END, A_END)
        self._consts()
        for l in layers:
            for ph in phases:
                getattr(self, "ph_" + ph)(l)
                self.S.barrier()
                self.ar.reset()
        self.S.barrier()
        self.S.emit()

    def tile(self, shape, dt, name="t", pers=False):
        ap = (self.pers if pers else self.ar).alloc(shape, dt, name)
        return ap, Res(name)

    def rot(self, n, shape, dt, name="r"):
        tiles = [self.tile(shape, dt, name) for _ in range(n)]
        st = {"i": 0}

        def nxt():
            t = tiles[st["i"] % n]
            st["i"] += 1
            return t
        return nxt

    def load(self, dst, rdst, src, rsrc=None, q="sp", **kw):
        self.S.dma(q, dst, src, reads=[rsrc or self.rIN], writes=[rdst], **kw)

    def store(self, dst, rdst, src, rsrc, q="pool", **kw):
        self.S.dma(q, dst, src, reads=[rsrc], uw=[rdst], **kw)

    def _consts(self):
        T = self.T
        self.C = {}
        for name, shape, dt in [("ident_f", [128, 128], F32), ("ident_b", [128, 128], BF16),
                                ("blkavg", [128, 128], F32), ("ropeP", [128, 128], F32),
                                ("hg_mask", [128, 512], F32), ("hg_rowm", [128, 2], F32), ("hg_cm", [128, 512], BF16), ("ch_c", [128, 128], BF16), ("ch_s", [128, 128], BF16)]:
            ap, r = self.tile(shape, dt, name, pers=True)
            self.load(ap, r, T[name])
            self.C[name] = (ap, r)
        for name in ["hg_dm", "hg_am"]:
            ap, r = self.tile([128, 2, 128], F32, name, pers=True)
            self.load(ap, r, T[name].rearrange("a p c -> p a c"))
            self.C[name] = (ap, r)
        ap, r = self.tile([128, 2, 32], F32, "hg_ind", pers=True)
        self.load(ap, r, T["hg_ind"].rearrange("a p c -> p a c"))
        self.C["hg_ind"] = (ap, r)
        ap, r = self.tile([64, 64], F32, "ones64", pers=True)
        self.load(ap, r, T["ones64"])
        self.C["ones64"] = (ap, r)
        ap, r = self.tile([128, 64], F32, "onesf", pers=True)
        self.S.op("pool", lambda e: e.memset(ap, 1.0), writes=[r])
        self.C["onesf"] = (ap, r)
        self.gs = [self.tile([128, D], F32, "gs", pers=True) for _ in range(2)]
        self.sh = [self.tile([128, D], F32, "sh", pers=True) for _ in range(2)]
        self.gt = [self.tile([128, D], F32, "gt", pers=True) for _ in range(2)]

    def ph_mod(self, l):
        S_, T, PS, rPS = self.S, self.T, self.PS, self.rPS
        cv, rcv = self.tile([128, 2, 8], F32, "cv")
        self.load(cv, rcv, T["cvec"])
        ca, rca = self.tile([128, 2, 8], F32, "ca")
        S_.op("act", lambda e: e.activation(out=ca, in_=cv, func=AF.Silu), reads=[rcv], writes=[rca])
        cab, rcab = self.tile([128, 2, 8, 128], F32, "cab")
        for s in range(2):
            S_.op("dve", lambda e, s=s: e.tensor_copy(out=cab[:, s], in_=ca[:, s].unsqueeze(2).broadcast_to([128, 8, 128])),
                  reads=[rca], writes=[rcab])
        bmb, rbmb = self.tile([128, 3 * D], F32, "bmb")
        self.load(bmb, rbmb, T["b_mod_b"][l])
        ngb, rngb = self.tile([128, D], F32, "ngb")
        self.load(ngb, rngb, T["norm_g_b"][l])
        modv = [self.tile([128, 3 * D], F32, "modv") for _ in range(2)]
        nxt = self.rot(2, [128, 8, 512], F32, "wm")
        wsrc = T["w_mod"][l].rearrange("(kt p) c -> p kt c", p=128)
        for blk in range(6):
            wm, rwm = nxt()
            self.load(wm, rwm, wsrc[:, :, blk * 512:(blk + 1) * 512])
            for s in range(2):
                pi = (blk * 2 + s) % 4
                for kt in range(8):
                    S_.op("pe", lambda e, s=s, kt=kt, pi=pi, wm=wm: e.matmul(PS[pi], lhsT=cab[:, s, kt, :], rhs=wm[:, kt, :],
                                                                          start=(kt == 0), stop=(kt == 7)),
                          reads=[rcab, rwm], writes=[rPS[pi]])
                mv, rmv = modv[s]
                S_.op("dve", lambda e, pi=pi, mv=mv, blk=blk: e.tensor_tensor(
                    out=mv[:, blk * 512:(blk + 1) * 512], in0=PS[pi], in1=bmb[:, blk * 512:(blk + 1) * 512], op=ALU.add),
                    reads=[rPS[pi], rbmb], writes=[rmv])
        for s in range(2):
            mv, rmv = modv[s]
            S_.op("act", lambda e, s=s, mv=mv: e.copy(out=self.sh[s][0], in_=mv[:, 0:D]), reads=[rmv], writes=[self.sh[s][1]])
            S_.op("dve", lambda e, s=s, mv=mv: e.scalar_tensor_tensor(out=self.gs[s][0], in0=mv[:, D:2 * D], scalar=1.0, in1=ngb,
                                                                      op0=ALU.add, op1=ALU.mult),
                  reads=[rmv, rngb], writes=[self.gs[s][1]])
            S_.op("act", lambda e, s=s, mv=mv: e.copy(out=self.gt[s][0], in_=mv[:, 2 * D:3 * D]), reads=[rmv], writes=[self.gt[s][1]])

    def rstd_of(self, ht, rht, junk, rjunk, ss, rss, rs, rrs):
        S_ = self.S
        S_.op("act", lambda e: e.activation(out=junk, in_=ht, func=AF.Square, accum_out=ss), reads=[rht], writes=[rjunk, rss])
        S_.op("act", lambda e: e.activation(out=ss, in_=ss, func=AF.Sqrt, scale=1.0 / D, bias=self.epsc[0]), reads=[rss, self.epsc[1]],
              writes=[rss])
        S_.op("dve", lambda e: e.reciprocal(out=rs, in_=ss), reads=[rss], writes=[rrs])

    def eps_const(self):
        if getattr(self, "epsc", None) is None:
            ap, r = self.tile([128, 1], F32, "eps", pers=True)
            self.S.op("pool", lambda e: e.memset(ap, EPS), writes=[r])
            self.epsc = (ap, r)

    def ph_proj(self, l):
        S_, T, PS, rPS = self.S, self.T, self.PS, self.rPS
        self.eps_const()
        idb, ridb = self.C["ident_b"]
        wbf, rwbf = self.tile([128, 8, 3584], BF16, "wbf")
        wsrc = T["w_in_p"][l].rearrange("(kt p) c -> p kt c", p=128)
        for kt in range(8):
            for hf in range(2):
                self.S.dma("pool", wbf[:, kt, hf * 1792:(hf + 1) * 1792], wsrc[:, kt, hf * 1792:(hf + 1) * 1792],
                           reads=[self.rIN], uw=[rwbf])
        src, rsrc = (T["xc"], self.rIN) if l == 0 else (self.H1, self.rH1)
        nht = self.rot(2, [128, D], F32, "ht")
        junk, rjunk = self.tile([128, D], F32, "junk")
        nss = self.rot(2, [128, 1], F32, "ss")
        nrs = self.rot(2, [128, 1], F32, "rs")
        nx1 = self.rot(2, [128, D], F32, "xm1")
        nxb = self.rot(2, [128, D], BF16, "xmb")
        nxT = self.rot(2, [128, 8, 512], BF16, "xT")
        nst = self.rot(4, [128, 512], F32, "st")
        nsb = self.rot(2, [128, 640], BF16, "sb")
        PSb = PS[7].bitcast(BF16)
        pcnt = [0]
        ecnt = [0]

        def evac(dst, rdst, src_ps, rsrc_ps):
            ecnt[0] += 1
            if ecnt[0] % 2:
                S_.op("act", lambda e: e.copy(out=dst, in_=src_ps), reads=[rsrc_ps], writes=[rdst])
            else:
                S_.op("dve", lambda e: e.tensor_copy(out=dst, in_=src_ps), reads=[rsrc_ps], writes=[rdst])

        for blk in range(9):
            tok0 = blk * 512
            ntile = 4 if blk < 8 else 2
            s = 0 if blk < 8 else 1
            N = ntile * 128
            xT, rxT = nxT()
            for ti in range(ntile):
                r0 = tok0 + ti * 128
                ht, rht = nht()
                self.load(ht, rht, src[r0:r0 + 128, :], rsrc)
                ss, rss = nss()
                rs, rrs = nrs()
                self.rstd_of(ht, rht, junk, rjunk, ss, rss, rs, rrs)
                x1, rx1 = nx1()
                S_.op("dve", lambda e, ht=ht, rs=rs, x1=x1, s=s: e.scalar_tensor_tensor(
                    out=x1, in0=ht, scalar=rs, in1=self.gs[s][0], op0=ALU.mult, op1=ALU.mult),
                    reads=[rht, rrs, self.gs[s][1]], writes=[rx1])
                xb, rxb = nxb()
                S_.op("pool", lambda e, x1=x1, xb=xb, s=s: e.tensor_tensor(out=xb, in0=x1, in1=self.sh[s][0], op=ALU.add),
                      reads=[rx1, self.sh[s][1]], writes=[rxb])
                for dt in range(8):
                    S_.op("pe", lambda e, dt=dt, xb=xb: e.transpose(out=PSb[:, dt * 128:(dt + 1) * 128],
                                                                   in_=xb[:, dt * 128:(dt + 1) * 128], identity=idb),
                          reads=[rxb, ridb], writes=[rPS[7]])
                evac(xT[:, :, ti * 128:(ti + 1) * 128], rxT, PSb.rearrange("p (a b) -> p a b", a=8), rPS[7])
            for ct in range(NF // 128):
                pi = pcnt[0] % 4
                pcnt[0] += 1
                for dt in range(8):
                    S_.op("pe", lambda e, dt=dt, ct=ct, pi=pi, xT=xT, N=N: e.matmul(
                        PS[pi][:, :N], lhsT=wbf[:, dt, ct * 128:(ct + 1) * 128], rhs=xT[:, dt, :N], start=(dt == 0), stop=(dt == 7)),
                        reads=[rwbf, rxT], writes=[rPS[pi]])
                st, rst = nst()
                evac(st[:, :N], rst, PS[pi][:, :N], rPS[pi])
                self.store(self.PJF[ct * 128:(ct + 1) * 128, tok0:tok0 + N], self.rPJF, st[:, :N], rst)
            for ti in range(ntile):
                r0 = tok0 + ti * 128
                for (c0, cn, kind) in [(NF, 512, 0), (NF + 512, 512, 1), (NF + 1024, 128, 2)]:
                    pi = pcnt[0] % 4
                    pcnt[0] += 1
                    for dt in range(8):
                        S_.op("pe", lambda e, dt=dt, pi=pi, xT=xT, ti=ti, c0=c0, cn=cn: e.matmul(
                            PS[pi][:, :cn], lhsT=xT[:, dt, ti * 128:(ti + 1) * 128], rhs=wbf[:, dt, c0:c0 + cn],
                            start=(dt == 0), stop=(dt == 7)), reads=[rwbf, rxT], writes=[rPS[pi]])
                    if kind == 0:
                        st, rst = nst()
                        evac(st, rst, PS[pi], rPS[pi])
                        self.store(self.PJTf[r0:r0 + 128, :], self.rPJTf, st, rst)
                    elif kind == 1:
                        sb, rsb = nsb()
                        evac(sb[:, 0:512], rsb, PS[pi], rPS[pi])
                    else:
                        evac(sb[:, 512:640], rsb, PS[pi][:, :128], rPS[pi])
                        self.store(self.PJTb[r0:r0 + 128, :], self.rPJTb, sb, rsb)

    def ph_conv(self, l):
        S_, T = self.S, self.T
        cw, rcw = self.tile([128, 2, 3], F32, "cw")
        self.load(cw, rcw, T["conv_w_c"][l])
        seqs = [(0, S)] + ([(S, L)] if l == 0 else [])
        CH = 2048
        for (t0, n) in seqs:
            for ct in range(2):
                for c0 in range(0, n, CH):
                    m = min(CH, n - c0)
                    a = t0 + c0
                    lo = 1 if c0 > 0 else 0
                    hi = 1 if c0 + m < n else 0
                    cc, rcc = self.tile([128, CH + 2], F32, "cc")
                    cv_, rcv = self.tile([128, CH + 2], F32, "cvv")
                    cb, rcb = self.tile([128, CH], F32, "cb")
                    cz, rcz = self.tile([128, CH], F32, "cz")
                    self.load(cc[:, 1 - lo:1 + m + hi], rcc, self.PJF[F_CC + ct * 128:F_CC + (ct + 1) * 128, a - lo:a + m + hi], self.rPJF)
                    self.load(cv_[:, 1 - lo:1 + m + hi], rcv, self.PJF[F_CV + ct * 128:F_CV + (ct + 1) * 128, a - lo:a + m + hi], self.rPJF)
                    self.load(cb[:, :m], rcb, self.PJF[F_CB + ct * 128:F_CB + (ct + 1) * 128, a:a + m], self.rPJF)
                    self.load(cz[:, :m], rcz, self.PJF[F_CZ + ct * 128:F_CZ + (ct + 1) * 128, a:a + m], self.rPJF)
                    p, rp = self.tile([128, CH + 2], F32, "p")
                    S_.op("pool", lambda e, p=p: e.memset(p, 0.0), writes=[rp])
                    S_.op("dve", lambda e, p=p, cc=cc, cv_=cv_, lo=lo, hi=hi, m=m: e.tensor_tensor(
                        out=p[:, 1 - lo:1 + m + hi], in0=cc[:, 1 - lo:1 + m + hi], in1=cv_[:, 1 - lo:1 + m + hi], op=ALU.mult),
                        reads=[rcc, rcv], writes=[rp])
                    acc, racc = self.tile([128, CH], F32, "acc")
                    S_.op("dve", lambda e, p=p, acc=acc, m=m, ct=ct, cw=cw: e.tensor_scalar(
                        out=acc[:, :m], in0=p[:, 0:m], scalar1=cw[:, ct, 0:1], scalar2=None, op0=ALU.mult),
                        reads=[rp, rcw], writes=[racc])
                    for k in (1, 2):
                        S_.op("dve", lambda e, p=p, acc=acc, m=m, ct=ct, k=k, cw=cw: e.scalar_tensor_tensor(
                            out=acc[:, :m], in0=p[:, k:k + m], scalar=cw[:, ct, k:k + 1], in1=acc[:, :m], op0=ALU.mult, op1=ALU.add),
                            reads=[rp, rcw, racc], writes=[racc])
                    S_.op("act", lambda e, cz=cz, m=m: e.activation(out=cz[:, :m], in_=cz[:, :m], func=AF.Silu), reads=[rcz], writes=[rcz])
                    S_.op("pool", lambda e, acc=acc, cb=cb, m=m: e.tensor_tensor(out=cb[:, :m], in0=acc[:, :m], in1=cb[:, :m], op=ALU.mult),
                          reads=[racc, rcb], writes=[rcb])
                    res, rres = self.tile([128, CH], BF16, "res")
                    S_.op("dve", lambda e, res=res, cb=cb, cz=cz, m=m: e.tensor_tensor(out=res[:, :m], in0=cb[:, :m], in1=cz[:, :m], op=ALU.mult),
                          reads=[rcb, rcz], writes=[rres])
                    self.store(self.MIX[ct * 128:(ct + 1) * 128, a:a + m], self.rMIX, res[:, :m], rres)
                    self.S.barrier()
                    self.ar.reset()
                    cw, rcw = self.tile([128, 2, 3], F32, "cw")
                    self.load(cw, rcw, T["conv_w_c"][l])

    def ph_att(self, l):
        S_, T, PS, rPS = self.S, self.T, self.PS, self.rPS
        self.eps_const()
        blk_, rblk = self.C["blkavg"]
        rp_, rrp = self.C["ropeP"]
        onesf, ronesf = self.C["onesf"]
        g2, rg2 = self.tile([128, 2], F32, "qkg")
        self.load(g2, rg2, T["qk_g_c"][l])
        cosT, rcos = self.tile([128, TT], F32, "cosT")
        sinT, rsin = self.tile([128, TT], F32, "sinT")
        self.load(cosT, rcos, T["cosT"])
        self.load(sinT, rsin, T["sinT"])
        qT, rqT = self.tile([128, 2, TT], BF16, "qT")
        KD, rKD = self.tile([128, 2, TT], BF16, "KD")
        VA, rVA = self.tile([128, NT, 2, 65], BF16, "VA")
        S_.op("pool", lambda e: e.memset(VA, 1.0), writes=[rVA])
        for g in range(2):
            self.S.dma("sp", VA[:, :, g, 0:64], self.PJTb[:, g * 64:(g + 1) * 64].rearrange("(n p) d -> p n d", p=128),
                       reads=[self.rPJTb], uw=[rVA])
        nx = self.rot(2, [128, 512], F32, "x")
        nsq = self.rot(2, [128, 512], F32, "sq")
        nri = self.rot(2, [128, 512], F32, "ri")
        nxh = self.rot(2, [128, 512], F32, "xh")
        nt1 = self.rot(2, [128, 512], F32, "t1")
        nt2 = self.rot(2, [128, 512], F32, "t2")
        for kind in range(4):
            for blk in range(9):
                tok0 = blk * 512
                N = 512 if blk < 8 else 256
                x, rx = nx()
                if kind < 2:
                    self.load(x[:, :N], rx, self.PJF[F_Q + kind * 128:F_Q + (kind + 1) * 128, tok0:tok0 + N], self.rPJF)
                    gcol = g2[:, 0:1]
                    dst = qT[:, kind, tok0:tok0 + N]
                    rdst = rqT
                else:
                    g = kind - 2
                    for j in range(2):
                        self.S.dma("sp", x[j * 64:(j + 1) * 64, :N], self.PJF[F_K + g * 64:F_K + (g + 1) * 64, tok0:tok0 + N],
                                   reads=[self.rPJF], uw=[rx])
                    gcol = g2[:, 1:2]
                    dst = KD[:, g, tok0:tok0 + N]
                    rdst = rKD
                sq, rsq = nsq()
                S_.op("pool", lambda e, x=x, sq=sq, N=N: e.tensor_tensor(out=sq[:, :N], in0=x[:, :N], in1=x[:, :N], op=ALU.mult),
                      reads=[rx], writes=[rsq])
                S_.op("pe", lambda e, sq=sq, N=N: e.matmul(PS[0][:, :N], lhsT=blk_, rhs=sq[:, :N], start=True, stop=True),
                      reads=[rsq, rblk], writes=[rPS[0]])
                ri, rri = nri()
                S_.op("act", lambda e, ri=ri, N=N: e.activation(out=ri[:, :N], in_=PS[0][:, :N], func=AF.Sqrt, bias=self.epsc[0]),
                      reads=[rPS[0], self.epsc[1]], writes=[rri])
                S_.op("dve", lambda e, ri=ri, N=N: e.reciprocal(out=ri[:, :N], in_=ri[:, :N]), reads=[rri], writes=[rri])
                xh, rxh = nxh()
                S_.op("dve", lambda e, x=x, xh=xh, ri=ri, N=N, gcol=gcol: e.scalar_tensor_tensor(
                    out=xh[:, :N], in0=x[:, :N], scalar=gcol, in1=ri[:, :N], op0=ALU.mult, op1=ALU.mult),
                    reads=[rx, rri, rg2], writes=[rxh])
                S_.op("pe", lambda e, xh=xh, N=N: e.matmul(PS[1][:, :N], lhsT=rp_, rhs=xh[:, :N], start=True, stop=True),
                      reads=[rxh, rrp], writes=[rPS[1]])
                t1, rt1 = nt1()
                S_.op("pool", lambda e, xh=xh, t1=t1, N=N, tok0=tok0: e.tensor_tensor(
                    out=t1[:, :N], in0=xh[:, :N], in1=cosT[:, tok0:tok0 + N], op=ALU.mult), reads=[rxh, rcos], writes=[rt1])
                t2, rt2 = nt2()
                S_.op("dve", lambda e, t2=t2, N=N, tok0=tok0: e.tensor_tensor(
                    out=t2[:, :N], in0=PS[1][:, :N], in1=sinT[:, tok0:tok0 + N], op=ALU.mult), reads=[rPS[1], rsin], writes=[rt2])
                S_.op("pool", lambda e, t1=t1, t2=t2, N=N, dst=dst: e.tensor_tensor(out=dst, in0=t1[:, :N], in1=t2[:, :N], op=ALU.add),
                      reads=[rt1, rt2], uw=[rdst])
        nPT = self.rot(4, [128, 512], BF16, "PT")
        ndr = self.rot(2, [128, 512], F32, "dr")
        nbc = self.rot(2, [64, 512], F32, "bc")
        naz = self.rot(2, [64, 512], F32, "az")
        no1 = self.rot(2, [64, 512], F32, "o1")
        nres = self.rot(2, [64, 512], BF16, "ares")
        qblocks = [(b * 512, 512, list(range(NT))) for b in range(8)]
        if l == 0:
            qblocks.append((S, L, [32, 33]))
        sti = [0]
        oti = [0]
        pending = [None]
        for g in range(2):
            for (tok0, N, keys) in qblocks:
                for j in range(2):
                    h = 2 * g + j
                    oi = 4 + (oti[0] % 2)
                    oti[0] += 1
                    OT, rOT = PS[oi], rPS[oi]
                    nk = len(keys)

                    def emit_pv(ki, kt, PT, rPT, OT=OT, rOT=rOT, g=g, N=N, nk=nk):
                        S_.op("pe", lambda e: e.matmul(OT[0:65, :N], lhsT=VA[:, kt, g, :], rhs=PT[:, :N], start=(ki == 0), stop=(ki == nk - 1)),
                              reads=[rVA, rPT], writes=[rOT])

                    prev = None
                    for ki, kt in enumerate(keys):
                        si = 2 * j + (sti[0] % 2)
                        sti[0] += 1
                        ST, rST = PS[si], rPS[si]
                        S_.op("pe", lambda e, ST=ST, j=j, g=g, kt=kt, tok0=tok0, N=N: e.matmul(
                            ST[:, :N], lhsT=KD[j * 64:(j + 1) * 64, g, kt * 128:(kt + 1) * 128], rhs=qT[j * 64:(j + 1) * 64, g, tok0:tok0 + N],
                            start=True, stop=True), reads=[rKD, rqT], writes=[rST])
                        PT, rPT = nPT()
                        S_.op("act", lambda e, ST=ST, PT=PT, N=N: e.activation(out=PT[:, :N], in_=ST[:, :N], func=AF.Exp, scale=0.125),
                              reads=[rST], writes=[rPT])
                        if prev is not None:
                            emit_pv(*prev)
                        prev = (ki, kt, PT, rPT)
                        if ki == 1 and pending[0] is not None:
                            pending[0]()
                            pending[0] = None
                    emit_pv(*prev)
                    if pending[0] is not None:
                        pending[0]()
                        pending[0] = None

                    def tail(OT=OT, rOT=rOT, N=N, h=h, tok0=tok0):
                        dr, rdr = ndr()
                        S_.op("dve", lambda e: e.reciprocal(out=dr[64:65, :N], in_=OT[64:65, :N]), reads=[rOT], writes=[rdr])
                        S_.op("pe", lambda e: e.matmul(PS[6][0:64, :N], lhsT=onesf[64:65, 0:64], rhs=dr[64:65, :N], start=True, stop=True),
                              reads=[rdr, ronesf], writes=[rPS[6]])
                        bc, rbc = nbc()
                        S_.op("dve", lambda e: e.tensor_copy(out=bc[:, :N], in_=PS[6][0:64, :N]), reads=[rPS[6]], writes=[rbc])
                        o1, ro1 = no1()
                        S_.op("dve", lambda e: e.tensor_tensor(out=o1[:, :N], in0=OT[0:64, :N], in1=bc[:, :N], op=ALU.mult),
                              reads=[rOT, rbc], writes=[ro1])
                        az, raz = naz()
                        self.load(az[:, :N], raz, self.PJF[F_AZ + h * 64:F_AZ + (h + 1) * 64, tok0:tok0 + N], self.rPJF)
                        S_.op("act", lambda e: e.activation(out=az[:, :N], in_=az[:, :N], func=AF.Silu), reads=[raz], writes=[raz])
                        res, rres = nres()
                        S_.op("pool", lambda e: e.tensor_tensor(out=res[:, :N], in0=o1[:, :N], in1=az[:, :N], op=ALU.mult),
                              reads=[ro1, raz], writes=[rres])
                        self.store(self.MIX[256 + h * 64:256 + (h + 1) * 64, tok0:tok0 + N], self.rMIX, res[:, :N], rres)
                    pending[0] = tail
        if pending[0] is not None:
            pending[0]()
            pending[0] = None

    def ph_four(self, l):
        S_, T, PS, rPS = self.S, self.T, self.PS, self.rPS
        chc, rchc = self.C["ch_c"]
        chs, rchs = self.C["ch_s"]
        seqs = [(0, S, T["dft_c"], T["dft_s"], 1.0 / 512)] + ([(S, L, T["dftc_c"], T["dftc_s"], 1.0 / 128)] if l == 0 else [])
        JC = 8
        ntc = self.rot(3, [128, JC, 512], BF16, "tc")
        nts = self.rot(3, [128, JC, 512], BF16, "ts")
        nw = self.rot(4, [128, 512], BF16, "w12")
        nfz = self.rot(2, [128, 512], F32, "fz")
        nres = self.rot(2, [128, 512], BF16, "fres")
        for (t0, n, TC, TS, scl) in seqs:
            nt = n // 128
            U, rU = self.tile([128, NT, 256], BF16, "U")
            usrc = self.PJTb[t0:t0 + n, 384:640].rearrange("(n p) c -> p n c", p=128)
            for u0 in range(0, nt, 4):
                u1 = min(nt, u0 + 4)
                self.S.dma("sp", U[:, u0:u1, :], usrc[:, u0:u1, :], reads=[self.rPJTb], uw=[rU])
            tcs = TC.rearrange("(j p) c -> p j c", p=128)
            tss = TS.rearrange("(j p) c -> p j c", p=128)
            for b0 in range(0, n, 512):
                N = min(512, n - b0)
                for j0 in range(0, nt, JC):
                    jn = min(JC, nt - j0)
                    tc, rtc = ntc()
                    ts, rts = nts()
                    self.load(tc[:, :jn, :N], rtc, tcs[:, j0:j0 + jn, b0:b0 + N])
                    self.load(ts[:, :jn, :N], rts, tss[:, j0:j0 + jn, b0:b0 + N])
                    for jj in range(jn):
                        j = j0 + jj
                        for half in range(2):
                            for w, (tb, rtb) in enumerate(((tc, rtc), (ts, rts))):
                                pi = half * 2 + w
                                S_.op("pe", lambda e, pi=pi, j=j, jj=jj, half=half, tb=tb, N=N, nt=nt, U=U: e.matmul(
                                    PS[pi][:, :N], lhsT=U[:, j, half * 128:(half + 1) * 128], rhs=tb[:, jj, :N],
                                    start=(j == 0), stop=(j == nt - 1)), reads=[rU, rtb], writes=[rPS[pi]])
                for half in range(2):
                    w1, rw1 = nw()
                    w2, rw2 = nw()
                    S_.op("act", lambda e, w1=w1, half=half, N=N: e.copy(out=w1[:, :N], in_=PS[half * 2][:, :N]), reads=[rPS[half * 2]], writes=[rw1])
                    S_.op("dve", lambda e, w2=w2, half=half, N=N: e.tensor_copy(out=w2[:, :N], in_=PS[half * 2 + 1][:, :N]),
                          reads=[rPS[half * 2 + 1]], writes=[rw2])
                    yi = 4 + half
                    S_.op("pe", lambda e, yi=yi, w1=w1, N=N: e.matmul(PS[yi][:, :N], lhsT=chc, rhs=w1[:, :N], start=True, stop=False),
                          reads=[rw1, rchc], writes=[rPS[yi]])
                    S_.op("pe", lambda e, yi=yi, w2=w2, N=N: e.matmul(PS[yi][:, :N], lhsT=chs, rhs=w2[:, :N], start=False, stop=True),
                          reads=[rw2, rchs], writes=[rPS[yi]])
                    fz, rfz = nfz()
                    self.load(fz[:, :N], rfz, self.PJF[F_FZ + half * 128:F_FZ + (half + 1) * 128, t0 + b0:t0 + b0 + N], self.rPJF)
                    S_.op("act", lambda e, fz=fz, N=N: e.activation(out=fz[:, :N], in_=fz[:, :N], func=AF.Silu), reads=[rfz], writes=[rfz])
                    res, rres = nres()
                    S_.op("dve", lambda e, res=res, yi=yi, fz=fz, N=N, scl=scl: e.scalar_tensor_tensor(
                        out=res[:, :N], in0=PS[yi][:, :N], scalar=scl, in1=fz[:, :N], op0=ALU.mult, op1=ALU.mult),
                        reads=[rPS[yi], rfz], writes=[rres])
                    self.store(self.MIX[768 + half * 128:768 + (half + 1) * 128, t0 + b0:t0 + b0 + N], self.rMIX, res[:, :N], rres)

    def ph_out(self, l):
        S_, T, PS, rPS = self.S, self.T, self.PS, self.rPS
        self.eps_const()
        wob, rwob = self.tile([128, 8, D], BF16, "wob")
        wsrc = T["w_out"][l].rearrange("(kt p) c -> p kt c", p=128)
        for kt in range(8):
            self.S.dma("pool", wob[:, kt, :], wsrc[:, kt, :], reads=[self.rIN], uw=[rwob])
        src, rsrc = (T["xc"], self.rIN) if l == 0 else (self.H1, self.rH1)
        last = (l == DEPTH - 1)
        if last:
            fg, rfg = self.tile([128, D], F32, "fg")
            self.load(fg, rfg, T["final_g_b"])
            junk, rjunk = self.tile([128, D], F32, "junk")
            nss = self.rot(2, [128, 1], F32, "ss")
            nrs = self.rot(2, [128, 1], F32, "rs")
            nob = self.rot(2, [128, D], F32, "ob")
        nmx = self.rot(2, [128, 8, 512], BF16, "mx")
        nht = self.rot(2, [128, D], F32, "ht")
        ntm = self.rot(2, [128, D], F32, "tm")
        nhn = self.rot(2, [128, D], F32, "hn")
        nblk = 8 if last else 9
        pc = [0]
        for blk in range(nblk):
            tok0 = blk * 512
            ntile = 4 if blk < 8 else 2
            s = 0 if blk < 8 else 1
            N = ntile * 128
            mx, rmx = nmx()
            self.load(mx[:, :, :N], rmx, self.MIX[:, tok0:tok0 + N].rearrange("(ct p) t -> p ct t", p=128), self.rMIX)
            for ti in range(ntile):
                r0 = tok0 + ti * 128
                ht, rht = nht()
                self.load(ht, rht, src[r0:r0 + 128, :], rsrc)
                tm, rtm = ntm()
                hn, rhn = nhn()
                for nb in range(2):
                    pi = pc[0] % 4
                    pc[0] += 1
                    for ct in range(8):
                        S_.op("pe", lambda e, pi=pi, ct=ct, mx=mx, ti=ti, nb=nb: e.matmul(
                            PS[pi], lhsT=mx[:, ct, ti * 128:(ti + 1) * 128], rhs=wob[:, ct, nb * 512:(nb + 1) * 512],
                            start=(ct == 0), stop=(ct == 7)), reads=[rmx, rwob], writes=[rPS[pi]])
                    S_.op("dve", lambda e, pi=pi, tm=tm, nb=nb, s=s: e.tensor_tensor(
                        out=tm[:, nb * 512:(nb + 1) * 512], in0=PS[pi], in1=self.gt[s][0][:, nb * 512:(nb + 1) * 512], op=ALU.mult),
                        reads=[rPS[pi], self.gt[s][1]], uw=[rtm])
                S_.op("pool", lambda e, tm=tm, hn=hn, ht=ht: e.tensor_tensor(out=hn, in0=tm, in1=ht, op=ALU.add),
                      reads=[rtm, rht], writes=[rhn])
                if not last:
                    self.store(self.H1[r0:r0 + 128, :], self.rH1, hn, rhn)
                else:
                    ss, rss = nss()
                    rs, rrs = nrs()
                    self.rstd_of(hn, rhn, junk, rjunk, ss, rss, rs, rrs)
                    ob, rob = nob()
                    S_.op("dve", lambda e, hn=hn, rs=rs, ob=ob: e.scalar_tensor_tensor(
                        out=ob, in0=hn, scalar=rs, in1=fg, op0=ALU.mult, op1=ALU.mult), reads=[rhn, rrs, rfg], writes=[rob])
                    self.store(self.out[r0:r0 + 128, :], self.rOUT, ob, rob)

    @staticmethod
    def _slot(d, i, cl):
        if i >= 32:
            cc = (i - 32) * 2 + cl
            return 1 + cc if d == 0 else 4 - cc
        c = 2 * i + cl
        return 5 + c if d == 0 else 68 - c

    def _gates(self, fx, rfx, W, lb, oml, rlb, nsg, ntmp, nlf, nkk):
        S_ = self.S
        sg, rsg = nsg()
        S_.op("act", lambda e: e.activation(out=sg[:, :W], in_=fx[:, :W], func=AF.Sigmoid), reads=[rfx], writes=[rsg])
        tmp, rtmp = ntmp()
        S_.op("dve", lambda e: e.tensor_tensor(out=tmp[:, :W], in0=sg[:, :W], in1=oml, op=ALU.mult), reads=[rsg, rlb], writes=[rtmp])
        S_.op("pool", lambda e: e.tensor_tensor(out=sg[:, :W], in0=tmp[:, :W], in1=lb, op=ALU.add), reads=[rtmp, rlb], writes=[rsg])
        lf, rlf = nlf()
        S_.op("act", lambda e: e.activation(out=lf[:, :W], in_=sg[:, :W], func=AF.Ln), reads=[rsg], writes=[rlf])
        kk, rkk = nkk()
        S_.op("pool", lambda e: e.tensor_tensor(out=kk[:, :W], in0=oml, in1=tmp[:, :W], op=ALU.subtract), reads=[rtmp, rlb], writes=[rkk])
        return lf, rlf, kk, rkk

    def ph_hg(self, l):
        S_, T, PS, rPS = self.S, self.T, self.PS, self.rPS
        self.eps_const()
        dm, rdm = self.C["hg_dm"]
        am, ram = self.C["hg_am"]
        ind, rind = self.C["hg_ind"]
        mask, rmask = self.C["hg_mask"]
        idf, ridf = self.C["ident_f"]
        ones64, rones64 = self.C["ones64"]
        lbt, rlb = self.tile([128, 2, 256], F32, "lbt")
        oml, _ = self.tile([128, 2, 256], F32, "oml")
        if l == 0:
            S_.op("pool", lambda e: e.memset(lbt, 0.0), writes=[rlb])
            S_.op("pool", lambda e: e.memset(oml, 1.0), writes=[rlb])
        else:
            lbp, rlbp = self.tile([128, 2, 2, 256], F32, "lbp")
            self.load(lbp, rlbp, T["lbp_b"])
            S_.op("act", lambda e: e.activation(out=lbp, in_=lbp, func=AF.Exp), reads=[rlbp], writes=[rlbp])
            den, rden = self.tile([128, 2, 256], F32, "den")
            S_.op("dve", lambda e: e.tensor_tensor(out=den, in0=lbp[:, :, 0, :], in1=lbp[:, :, 1, :], op=ALU.add), reads=[rlbp], writes=[rden])
            S_.op("dve", lambda e: e.reciprocal(out=den, in_=den), reads=[rden], writes=[rden])
            S_.op("dve", lambda e: e.tensor_tensor(out=lbt, in0=lbp[:, :, 1, :], in1=den, op=ALU.mult), reads=[rlbp, rden], writes=[rlb])
            S_.op("dve", lambda e: e.tensor_tensor(out=oml, in0=lbp[:, :, 0, :], in1=den, op=ALU.mult), reads=[rlbp, rden], writes=[rlb])
        lbf = lbt.rearrange("p a c -> p (a c)")
        omf = oml.rearrange("p a c -> p (a c)")
        hgg, rhgg = self.tile([64, 4], F32, "hgg")
        self.load(hgg, rhgg, T["hg_g_c"][l])
        SR = [self.tile([128, 2, 68, 64], BF16, "SR") for _ in range(2)]
        nfx = self.rot(3, [128, 512], F32, "fx")
        nsg = self.rot(3, [128, 512], F32, "sg")
        ntmp = self.rot(3, [128, 512], F32, "tmp")
        nlf = self.rot(3, [128, 512], F32, "lf")
        nkk = self.rot(3, [128, 512], F32, "kk")
        nV = self.rot(3, [128, 256], BF16, "V")
        mark = self.ar.cur
        UPD, rUPD = self.tile([128, 2, 64, 69], F32, "UPD")
        Sst, rSst = self.tile([128, 2, 64, 69], F32, "Sst")
        ER, rER = self.tile([128, 2, 69], F32, "ER")
        ASUM, rASUM = self.tile([128, 2, 69], F32, "ASUM")
        RSUM, rRSUM = self.tile([128, 2, 69], F32, "RSUM")
        EA, rEA = self.tile([128, 2, 69], F32, "EA")
        nE = self.rot(3, [128, 256], F32, "E")
        nkl = self.rot(3, [128, 256], BF16, "kl")
        cnt = [0]

        def cp(out, rout, in_, rin, uw=False):
            cnt[0] += 1
            kw = dict(reads=[rin], uw=[rout]) if uw else dict(reads=[rin], writes=[rout])
            if cnt[0] % 2:
                S_.op("act", lambda e: e.copy(out=out, in_=in_), **kw)
            else:
                S_.op("dve", lambda e: e.tensor_copy(out=out, in_=in_), **kw)

        for d in range(2):
            S_.op("pool", lambda e: e.memset(UPD, 0.0), writes=[rUPD])
            S_.op("pool", lambda e: e.memset(ASUM, 0.0), writes=[rASUM])
            S_.op("pool", lambda e: e.memset(RSUM, 0.0), writes=[rRSUM])
            def p1A(i, d=d):
                r0 = i * 128
                pa, pb = 0, 1
                fx, rfx = nfx()
                self.load(fx[:, :256], rfx, self.PJTf[r0:r0 + 128, d * 256:(d + 1) * 256], self.rPJTf)
                V, rV = nV()
                self.load(V, rV, self.PJTb[r0:r0 + 128, 128:384], self.rPJTb)
                lf, rlf, kk, rkk = self._gates(fx, rfx, 256, lbf[:, d * 256:(d + 1) * 256], omf[:, d * 256:(d + 1) * 256], rlb,
                                               nsg, ntmp, nlf, nkk)
                S_.op("pe", lambda e: e.matmul(PS[pa][:, :256], lhsT=dm[:, d, :], rhs=lf[:, :256], start=True, stop=True),
                      reads=[rlf, rdm], writes=[rPS[pa]])
                for half in range(2):
                    S_.op("pe", lambda e, half=half: e.matmul(
                        PS[pb][:, half * 32:(half + 1) * 32], lhsT=lf[:, half * 128:(half + 1) * 128], rhs=ind[:, d, :], start=True, stop=True),
                        reads=[rlf, rind], writes=[rPS[pb]])
                return dict(i=i, pa=pa, pb=pb, V=V, rV=rV, kk=kk, rkk=rkk)

            def p1B(c, d=d):
                i, pa, pb, V, rV, kk, rkk = c["i"], c["pa"], c["pb"], c["V"], c["rV"], c["kk"], c["rkk"]
                E, rE = nE()
                S_.op("act", lambda e: e.activation(out=E, in_=PS[pa][:, :256], func=AF.Exp), reads=[rPS[pa]], writes=[rE])
                kl, rkl = nkl()
                S_.op("dve", lambda e: e.tensor_tensor(out=kl, in0=kk[:, :256], in1=E, op=ALU.mult), reads=[rkk, rE], writes=[rkl])
                p1 = PS[pb][:, 0:64].rearrange("p (a b) -> p a b", a=2)
                for cl in range(2):
                    sl = self._slot(d, i, cl)
                    cp(ASUM[:, :, sl:sl + 1], rASUM, p1[:, :, cl:cl + 1], rPS[pb], uw=True)
                    cp(RSUM[:, :, sl:sl + 1], rRSUM, p1[:, :, 2 + cl:3 + cl], rPS[pb], uw=True)
                for cl in range(2):
                    for pair in range(2):
                        S_.op("pe", lambda e, cl=cl, pair=pair: e.matmul(
                            PS[2 + cl][:, pair * 256:(pair + 1) * 256], lhsT=kl[cl * 64:(cl + 1) * 64, pair * 128:(pair + 1) * 128],
                            rhs=V[cl * 64:(cl + 1) * 64, :], start=True, stop=True), reads=[rkl, rV], writes=[rPS[2 + cl]])
                for cl in range(2):
                    sl = self._slot(d, i, cl)
                    for pair in range(2):
                        for hl in range(2):
                            c0 = pair * 256 + (pair * 2 + hl) * 64
                            cp(UPD[hl * 64:(hl + 1) * 64, pair, :, sl:sl + 1],
                               rUPD, PS[2 + cl][hl * 64:(hl + 1) * 64, c0:c0 + 64].unsqueeze(2), rPS[2 + cl], uw=True)

            for step in range(NT):
                p1B(p1A(step))
            S_.op("act", lambda e: e.activation(out=EA, in_=ASUM, func=AF.Exp), reads=[rASUM], writes=[rEA])
            S_.op("pool", lambda e: e.memset(Sst, 0.0), writes=[rSst])
            for sl in range(1, 69):
                for pair in range(2):
                    S_.op("dve", lambda e, sl=sl, pair=pair: e.scalar_tensor_tensor(
                        out=Sst[:, pair, :, sl:sl + 1], in0=Sst[:, pair, :, sl - 1:sl], scalar=EA[:, pair, sl:sl + 1],
                        in1=UPD[:, pair, :, sl:sl + 1], op0=ALU.mult, op1=ALU.add), reads=[rSst, rEA, rUPD], writes=[rSst])
            S_.op("act", lambda e: e.activation(out=ER, in_=RSUM, func=AF.Exp), reads=[rRSUM], writes=[rER])
            for pair in range(2):
                S_.op("dve", lambda e, pair=pair, d=d: e.tensor_tensor(
                    out=SR[d][0][:, pair], in0=Sst[:, pair, :, 0:68].rearrange("p v s -> p s v"),
                    in1=ER[:, pair, 1:69].unsqueeze(2).broadcast_to([128, 68, 64]), op=ALU.mult),
                    reads=[rSst, rER], writes=[SR[d][1]])
        self.S.barrier()
        self.ar.cur = mark
        nqh = self.rot(3, [128, 2, 128], F32, "qh")
        nhz = self.rot(3, [64, 4, 128], F32, "hz")
        nEQ = self.rot(2, [128, 512], F32, "EQ")
        nEK = self.rot(2, [128, 512], F32, "EK")
        nqa = self.rot(3, [128, 4, 128], BF16, "qa")
        nkb = self.rot(3, [128, 4, 128], BF16, "kb")
        nkz = self.rot(3, [128, 4, 128], BF16, "kz")
        nqz = self.rot(6, [128, 4, 128], BF16, "qz")
        cm, rcm = self.C["hg_cm"]
        rowm, rrowm = self.C["hg_rowm"]
        mask4 = mask.rearrange("p (d a t) -> p d a t", d=2, a=2)
        SCB = [[2, 3], [6, 7]]
        scs = [[self.tile([128, 512], BF16, "sc") for _ in range(2)] for _ in range(3)]
        nsq = self.rot(2, [64, 512], F32, "hsq")
        nri = self.rot(2, [64, 512], F32, "hri")
        no1 = self.rot(2, [64, 512], F32, "ho1")
        nres = self.rot(2, [64, 4, 128], BF16, "hres")
        tiles = list(range(32)) + ([32, 33] if l == 0 else [])

        def p2A(n_, i):
            r0 = i * 128
            fx, rfx = nfx()
            self.load(fx, rfx, self.PJTf[r0:r0 + 128, :], self.rPJTf)
            V, rV = nV()
            self.load(V, rV, self.PJTb[r0:r0 + 128, 128:384], self.rPJTb)
            qh, rqh = nqh()
            self.load(qh, rqh, self.PJF[F_HQ:F_HQ + 256, r0:r0 + 128].rearrange("(a p) t -> p a t", p=128), self.rPJF)
            S_.op("act", lambda e: e.activation(out=qh, in_=qh, func=AF.Silu), reads=[rqh], writes=[rqh])
            hz, rhz = nhz()
            self.load(hz, rhz, self.PJF[F_HZ:F_HZ + 256, r0:r0 + 128].rearrange("(h v) t -> v h t", v=64), self.rPJF)
            S_.op("act", lambda e: e.activation(out=hz, in_=hz, func=AF.Silu), reads=[rhz], writes=[rhz])
            lf, rlf, kk, rkk = self._gates(fx, rfx, 512, lbf, omf, rlb, nsg, ntmp, nlf, nkk)
            for d in range(2):
                for half in range(2):
                    c0 = (d * 2 + half) * 128
                    S_.op("pe", lambda e, c0=c0, d=d: e.matmul(PS[0][:, c0:c0 + 128], lhsT=lf[:, c0:c0 + 128], rhs=am[:, d, :],
                                                               start=True, stop=True), reads=[rlf, ram], writes=[rPS[0]])
                    S_.op("pe", lambda e, c0=c0: e.matmul(PS[1][:, c0:c0 + 128], lhsT=kk[:, c0:c0 + 128], rhs=idf,
                                                          start=True, stop=True), reads=[rkk, ridf], writes=[rPS[1]])
            EQ, rEQ = nEQ()
            EK, rEK = nEK()
            S_.op("act", lambda e: e.activation(out=EQ, in_=PS[0], func=AF.Exp), reads=[rPS[0]], writes=[rEQ])
            S_.op("act", lambda e: e.activation(out=EK, in_=PS[0], func=AF.Exp, scale=-1.0), reads=[rPS[0]], writes=[rEK])
            qa, rqa = nqa()
            kb, rkb = nkb()
            kz, rkz = nkz()
            for d in range(2):
                S_.op("pool", lambda e, d=d: e.tensor_tensor(
                    out=qa[:, d * 2:d * 2 + 2, :], in0=EQ[:, d * 256:(d + 1) * 256].rearrange("p (a b) -> p a b", a=2), in1=qh, op=ALU.mult),
                    reads=[rEQ, rqh], uw=[rqa])
            qz = []
            for hl in range(2):
                q_, rq_ = nqz()
                S_.op("pool", lambda e, q_=q_, hl=hl: e.tensor_scalar(
                    out=q_.rearrange("p a b -> p (a b)"), in0=qa.rearrange("p a b -> p (a b)"), scalar1=rowm[:, hl:hl + 1], scalar2=None,
                    op0=ALU.mult), reads=[rqa, rrowm], writes=[rq_])
                qz.append((q_, rq_))
            S_.op("dve", lambda e: e.tensor_tensor(out=kb.rearrange("p a b -> p (a b)"), in0=PS[1], in1=EK, op=ALU.mult),
                  reads=[rPS[1], rEK], writes=[rkb])
            S_.op("pool", lambda e: e.tensor_tensor(out=kz.rearrange("p a b -> p (a b)"), in0=kb.rearrange("p a b -> p (a b)"),
                                                    in1=cm, op=ALU.mult), reads=[rkb, rcm], writes=[rkz])
            return dict(n_=n_, i=i, V=V, rV=rV, hz=hz, rhz=rhz, qa=qa, rqa=rqa, kb=kb, rkb=rkb, kz=kz, rkz=rkz, qz=qz)

        def p2B(c):
            qa, rqa, kb, rkb, kz, rkz = c["qa"], c["rqa"], c["kb"], c["rkb"], c["kz"], c["rkz"]
            for d in range(2):
                for h in range(4):
                    pair, hl = h // 2, h % 2
                    dp = d * 2 + pair
                    bk = SCB[d][hl]
                    for cl in range(2):
                        c0 = pair * 128 + cl * 64
                        t0_ = cl * 64
                        ta = (32, 64) if d == 0 else (0, 32)
                        tb_ = (0, 32) if d == 0 else (32, 64)
                        S_.op("pe", lambda e, bk=bk, hl=hl, dp=dp, c0=c0, t0_=t0_, ta=ta: e.matmul(
                            PS[bk][:, c0 + ta[0]:c0 + ta[1]], lhsT=kb[hl * 64:(hl + 1) * 64, dp, :],
                            rhs=qa[hl * 64:(hl + 1) * 64, dp, t0_ + ta[0]:t0_ + ta[1]], start=True, stop=True),
                            reads=[rkb, rqa], writes=[rPS[bk]])
                        S_.op("pe", lambda e, bk=bk, hl=hl, dp=dp, c0=c0, t0_=t0_, tb_=tb_: e.matmul(
                            PS[bk][:, c0 + tb_[0]:c0 + tb_[1]], lhsT=kz[hl * 64:(hl + 1) * 64, dp, :],
                            rhs=qa[hl * 64:(hl + 1) * 64, dp, t0_ + tb_[0]:t0_ + tb_[1]], start=True, stop=True),
                            reads=[rkz, rqa], writes=[rPS[bk]])
            sc = scs[c["n_"] % 3]
            c["sc"] = sc
            for d in range(2):
                for hl in range(2):
                    bk = SCB[d][hl]
                    S_.op("dve", lambda e, d=d, hl=hl, bk=bk: e.tensor_tensor(
                        out=sc[d][0].rearrange("p (a b t) -> p a b t", a=2, b=2)[:, :, hl, :],
                        in0=PS[bk][:, 0:256].rearrange("p (a t) -> p a t", a=2), in1=mask4[:, d, :, :], op=ALU.mult),
                        reads=[rPS[bk], rmask], uw=[sc[d][1]])

        def p2C(c):
            i, V, rV, hz, rhz, qz, sc = c["i"], c["V"], c["rV"], c["hz"], c["rhz"], c["qz"], c["sc"]
            r0 = i * 128
            for h in range(4):
                pair, hl = h // 2, h % 2
                for d in range(2):
                    S_.op("pe", lambda e, h=h, d=d: e.matmul(
                        PS[4][0:64, h * 128:(h + 1) * 128], lhsT=V[:, h * 64:(h + 1) * 64], rhs=sc[d][0][:, h * 128:(h + 1) * 128],
                        start=(d == 0), stop=False), reads=[rV, sc[d][1]], writes=[rPS[4]])
                for cl in range(2):
                    for d in range(2):
                        sl = self._slot(d, i, cl)
                        S_.op("pe", lambda e, h=h, d=d, cl=cl, sl=sl, pair=pair, hl=hl: e.matmul(
                            PS[4][0:64, h * 128 + cl * 64:h * 128 + (cl + 1) * 64], lhsT=SR[d][0][:, pair, sl - 1, :],
                            rhs=qz[hl][0][:, d * 2 + pair, cl * 64:(cl + 1) * 64], start=False, stop=(cl == 1 and d == 1)),
                            reads=[SR[d][1], qz[hl][1]], writes=[rPS[4]])
            sq, rsq = nsq()
            S_.op("act", lambda e: e.activation(out=sq, in_=PS[4][0:64, :], func=AF.Square), reads=[rPS[4]], writes=[rsq])
            S_.op("pe", lambda e: e.matmul(PS[5][0:64, :], lhsT=ones64, rhs=sq, start=True, stop=True),
                  reads=[rsq, rones64], writes=[rPS[5]])
            ri, rri = nri()
            S_.op("act", lambda e: e.activation(out=ri, in_=PS[5][0:64, :], func=AF.Sqrt, bias=self.epsc[0][0:64, :]),
                  reads=[rPS[5], self.epsc[1]], writes=[rri])
            S_.op("dve", lambda e: e.reciprocal(out=ri, in_=ri), reads=[rri], writes=[rri])
            o1, ro1 = no1()
            S_.op("dve", lambda e: e.tensor_tensor(out=o1, in0=PS[4][0:64, :], in1=ri, op=ALU.mult),
                  reads=[rPS[4], rri], writes=[ro1])
            res, rres = nres()
            for h in range(4):
                S_.op("dve", lambda e, h=h: e.scalar_tensor_tensor(
                    out=res[:, h, :], in0=o1[:, h * 128:(h + 1) * 128], scalar=hgg[:, h:h + 1], in1=hz[:, h, :], op0=ALU.mult, op1=ALU.mult),
                    reads=[ro1, rhz, rhgg], uw=[rres])
            self.store(self.MIX[512:768, r0:r0 + 128].rearrange("(h v) t -> v h t", v=64), self.rMIX, res, rres)

        nt2 = len(tiles)
        ctxs = {}
        for step in range(nt2 + 2):
            if step < nt2:
                ctxs[step] = p2A(step, tiles[step])
            if 0 <= step - 1 < nt2:
                p2B(ctxs[step - 1])
            if 0 <= step - 2 < nt2:
                p2C(ctxs.pop(step - 2))


_PROG = {}


def kernel(**inputs):
    maps = prepare_inputs({k: np.asarray(v) for k, v in inputs.items()})
    if "p" not in _PROG:
        _PROG["p"] = Prog()
    res = run_bass_kernel_spmd(_PROG["p"].nc, maps, core_ids=list(range(8)))
    return np.stack([np.asarray(r["out"], dtype=np.float32) for r in res.results], axis=0)
```
